# Optimizing a Trainium2 kernel written in Bass

```python
import jax, jax.numpy as jnp
from jax import lax
import numpy as np

D_MODEL = 2048
BATCH = 4
SEQ = 4096
DEPTH = 2
DEC_BATCH = 16
DEC_SEQ = 64
PAST_LEN = 1024

CHUNK = 64
N_A_LAYERS = DEPTH // 2
N_B_LAYERS = DEPTH - N_A_LAYERS
D_FF = ((8 * D_MODEL // 3 + 255) // 256) * 256
D_RNN = D_MODEL * 5 // 4
RNN_BLOCK = 256
N_RNN_HEADS = D_RNN // RNN_BLOCK
CONV_W = 4
LRU_C = 8.0
HEAD_DIM = 128
N_HEADS = D_MODEL // HEAD_DIM
ATTN_SCALE = HEAD_DIM ** -0.5
Q_BLOCK = 128
RMS_EPS = 1e-6
FORGET_BIAS_INIT = 3.0

kernel_name = 'hybrid_rglru_fox_yoco_step'


def rms_norm(x, g):
    xf = x.astype(jnp.float32)
    y = xf * lax.rsqrt(jnp.mean(xf * xf, axis=-1, keepdims=True) + RMS_EPS)
    return (y * g.astype(jnp.float32)).astype(x.dtype)


def swiglu_ffn(x, g, w_in, w_out):
    gate, up = jnp.split(rms_norm(x, g) @ w_in, 2, axis=-1)
    return (jax.nn.silu(gate) * up) @ w_out


def causal_conv(u, buf, w, b):
    t = u.shape[1]
    up = jnp.concatenate([buf.astype(u.dtype), u], axis=1)
    out = b
    for k in range(CONV_W):
        out = out + up[:, k:k + t] * w[k]
    return out, up[:, t:]


def rg_lru(x, h0, gate_w, gate_b, lam):
    b, t, _ = x.shape
    xf = x.astype(jnp.float32)
    xb = xf.reshape(b, t, N_RNN_HEADS, RNN_BLOCK)
    gates = jnp.einsum('bthi,ghij->gbthj', xb, gate_w.astype(jnp.float32)).reshape(2, b, t, D_RNN)
    gates = gates + gate_b.astype(jnp.float32)[:, None, None, :]
    r = jax.nn.sigmoid(gates[0])
    i = jax.nn.sigmoid(gates[1])
    log_a = -LRU_C * r * jax.nn.softplus(-lam.astype(jnp.float32))
    a = jnp.exp(log_a)
    inp = jnp.sqrt(-jnp.expm1(2.0 * log_a)) * (i * xf)
    inp = inp.at[:, 0].add(a[:, 0] * h0.astype(jnp.float32))

    def combine(c1, c2):
        a1, b1 = c1
        a2, b2 = c2
        return a1 * a2, a2 * b1 + b2

    _, h = lax.associative_scan(combine, (a, inp), axis=1)
    return h.astype(x.dtype), h[:, -1].astype(x.dtype)


def rglru_layer(x, conv_buf, h0, norm, w_in, conv_w, conv_b, gate_w, gate_b, lam, w_out):
    xn = rms_norm(x, norm)
    gate, u = jnp.split(xn @ w_in, 2, axis=-1)
    u, new_buf = causal_conv(u, conv_buf, conv_w, conv_b)
    h, h_last = rg_lru(u, h0, gate_w, gate_b, lam)
    return (h * jax.nn.gelu(gate)) @ w_out, new_buf, h_last


def shared_kv(x, kv_norm, w_kv, w_f, b_f):
    b, t = x.shape[:2]
    xn = rms_norm(x, kv_norm)
    k, v = jnp.split(xn @ w_kv, 2, axis=-1)
    logf = jax.nn.log_sigmoid((xn @ w_f).astype(jnp.float32) + b_f.astype(jnp.float32))
    return (k.reshape(b, t, N_HEADS, HEAD_DIM), v.reshape(b, t, N_HEADS, HEAD_DIM), logf)


def fox_attend(q, k, v, cq, ck, q_pos, k_pos):
    s = jnp.einsum('bqhd,bkhd->bhqk', q, k).astype(jnp.float32) * ATTN_SCALE
    s = s + jnp.swapaxes(cq, 1, 2)[:, :, :, None] - jnp.swapaxes(ck, 1, 2)[:, :, None, :]
    s = jnp.where(k_pos[None, None, None, :] <= q_pos[None, None, :, None], s, -jnp.inf)
    p = jax.nn.softmax(s, axis=-1)
    return jnp.einsum('bhqk,bkhd->bqhd', p.astype(v.dtype), v)


def fox_prompt(q, k, v, logf):
    b, s, h, d = q.shape
    c = jnp.cumsum(logf, axis=1)
    nb = s // Q_BLOCK
    qb = jnp.swapaxes(q.reshape(b, nb, Q_BLOCK, h, d), 0, 1)
    cqb = jnp.swapaxes(c.reshape(b, nb, Q_BLOCK, h), 0, 1)
    k_pos = jnp.arange(s)

    def one_block(args):
        qi, cqi, bi = args
        q_pos = bi * Q_BLOCK + jnp.arange(Q_BLOCK)
        return fox_attend(qi, k, v, cqi, c, q_pos, k_pos)

    o = lax.map(one_block, (qb, cqb, jnp.arange(nb)))
    return jnp.swapaxes(o, 0, 1).reshape(b, s, h, d)


def fox_sample(q, k_new, v_new, logf_new, cache_k, cache_v, cache_logf):
    past = cache_k.shape[1]
    t = q.shape[1]
    k = jnp.concatenate([cache_k, k_new], axis=1)
    v = jnp.concatenate([cache_v, v_new], axis=1)
    c = jnp.cumsum(jnp.concatenate([cache_logf.astype(jnp.float32), logf_new], axis=1), axis=1)
    return fox_attend(q, k, v, c[:, past:], c, past + jnp.arange(t), jnp.arange(past + t))


def fox_query(x, norm, w_qg):
    b, t = x.shape[:2]
    q, g = jnp.split(rms_norm(x, norm) @ w_qg, 2, axis=-1)
    return q.reshape(b, t, N_HEADS, HEAD_DIM), g


def fox_output(o, g, w_o):
    b, t = o.shape[:2]
    return (o.reshape(b, t, N_HEADS * HEAD_DIM) * jax.nn.sigmoid(g)) @ w_o


def setup_inputs(seed: int = 0) -> dict:
    key = jax.random.key(seed)
    ks = jax.random.split(key, 28)
    f32 = jnp.float32

    def nrm(k, shape, scale=1.0):
        return jax.random.normal(k, shape, f32) * scale

    a0 = jax.random.uniform(ks[14], (N_A_LAYERS, D_RNN), f32, 0.9, 0.999)
    p = a0 ** (1.0 / LRU_C)
    return {
        'x_prompt': nrm(ks[0], (BATCH, SEQ, D_MODEL)),
        'x_sample': nrm(ks[1], (DEC_BATCH, DEC_SEQ, D_MODEL)),
        'state_conv': nrm(ks[2], (N_A_LAYERS, DEC_BATCH, CONV_W - 1, D_RNN)),
        'state_h': nrm(ks[3], (N_A_LAYERS, DEC_BATCH, D_RNN), 0.5),
        'cache_k': nrm(ks[4], (DEC_BATCH, PAST_LEN, N_HEADS, HEAD_DIM)),
        'cache_v': nrm(ks[5], (DEC_BATCH, PAST_LEN, N_HEADS, HEAD_DIM)),
        'cache_logf': jax.nn.log_sigmoid(FORGET_BIAS_INIT + nrm(ks[6], (DEC_BATCH, PAST_LEN, N_HEADS), 0.5)),
        'ffn_norm': 1.0 + nrm(ks[7], (DEPTH, 2, D_MODEL), 0.05),
        'ffn_w_in': nrm(ks[8], (DEPTH, 2, D_MODEL, 2 * D_FF), D_MODEL ** -0.5),
        'ffn_w_out': nrm(ks[9], (DEPTH, 2, D_FF, D_MODEL), D_FF ** -0.5),
        'a_norm': 1.0 + nrm(ks[10], (N_A_LAYERS, D_MODEL), 0.05),
        'a_w_in': nrm(ks[11], (N_A_LAYERS, D_MODEL, 2 * D_RNN), D_MODEL ** -0.5),
        'a_conv_w': nrm(ks[12], (N_A_LAYERS, CONV_W, D_RNN), CONV_W ** -0.5),
        'a_conv_b': nrm(ks[13], (N_A_LAYERS, D_RNN), 0.01),
        'a_gate_w': nrm(ks[15], (N_A_LAYERS, 2, N_RNN_HEADS, RNN_BLOCK, RNN_BLOCK), RNN_BLOCK ** -0.5),
        'a_gate_b': nrm(ks[16], (N_A_LAYERS, 2, D_RNN), 0.01),
        'a_lambda': jnp.log(p) - jnp.log1p(-p),
        'a_w_out': nrm(ks[17], (N_A_LAYERS, D_RNN, D_MODEL), D_RNN ** -0.5),
        'kv_norm': 1.0 + nrm(ks[18], (D_MODEL,), 0.05),
        'w_kv': nrm(ks[19], (D_MODEL, 2 * D_MODEL), D_MODEL ** -0.5),
        'w_f': nrm(ks[20], (D_MODEL, N_HEADS), 0.5 * D_MODEL ** -0.5),
        'b_f': FORGET_BIAS_INIT + nrm(ks[21], (N_HEADS,), 0.1),
        'b_norm': 1.0 + nrm(ks[22], (N_B_LAYERS, D_MODEL), 0.05),
        'b_w_qg': nrm(ks[23], (N_B_LAYERS, D_MODEL, 2 * D_MODEL), D_MODEL ** -0.5),
        'b_w_o': nrm(ks[24], (N_B_LAYERS, D_MODEL, D_MODEL), D_MODEL ** -0.5),
        'final_norm': 1.0 + nrm(ks[25], (D_MODEL,), 0.05),
    }


def reference(x_prompt, x_sample, state_conv, state_h, cache_k, cache_v, cache_logf,
              ffn_norm, ffn_w_in, ffn_w_out, a_norm, a_w_in, a_conv_w, a_conv_b, a_gate_w,
              a_gate_b, a_lambda, a_w_out, kv_norm, w_kv, w_f, b_f, b_norm, b_w_qg, b_w_o,
              final_norm):
    xp, xs = x_prompt, x_sample
    bp = x_prompt.shape[0]
    conv_p, h_p, conv_s, h_s = [], [], [], []
    for l in range(DEPTH):
        xp = xp + 0.5 * swiglu_ffn(xp, ffn_norm[l, 0], ffn_w_in[l, 0], ffn_w_out[l, 0])
        xs = xs + 0.5 * swiglu_ffn(xs, ffn_norm[l, 0], ffn_w_in[l, 0], ffn_w_out[l, 0])
        if l < N_A_LAYERS:
            buf0 = jnp.zeros((bp, CONV_W - 1, D_RNN), xp.dtype)
            h0 = jnp.zeros((bp, D_RNN), xp.dtype)
            yp, cp, hp = rglru_layer(xp, buf0, h0, a_norm[l], a_w_in[l], a_conv_w[l], a_conv_b[l],
                                     a_gate_w[l], a_gate_b[l], a_lambda[l], a_w_out[l])
            ys, cs, hs = rglru_layer(xs, state_conv[l], state_h[l], a_norm[l], a_w_in[l], a_conv_w[l],
                                     a_conv_b[l], a_gate_w[l], a_gate_b[l], a_lambda[l], a_w_out[l])
            conv_p.append(cp)
            h_p.append(hp)
            conv_s.append(cs)
            h_s.append(hs)
        else:
            j = l - N_A_LAYERS
            qp, gp = fox_query(xp, b_norm[j], b_w_qg[j])
            yp = fox_output(fox_prompt(qp, kp, vp, fp), gp, b_w_o[j])
            qs, gs = fox_query(xs, b_norm[j], b_w_qg[j])
            ys = fox_output(fox_sample(qs, ks_, vs_, fs_, cache_k, cache_v, cache_logf), gs, b_w_o[j])
        xp = xp + yp
        xs = xs + ys
        xp = xp + 0.5 * swiglu_ffn(xp, ffn_norm[l, 1], ffn_w_in[l, 1], ffn_w_out[l, 1])
        xs = xs + 0.5 * swiglu_ffn(xs, ffn_norm[l, 1], ffn_w_in[l, 1], ffn_w_out[l, 1])
        if l == N_A_LAYERS - 1:
            kp, vp, fp = shared_kv(xp, kv_norm, w_kv, w_f, b_f)
            ks_, vs_, fs_ = shared_kv(xs, kv_norm, w_kv, w_f, b_f)
    y_prompt = rms_norm(xp, final_norm)
    y_sample = rms_norm(xs, final_norm)
    return (y_prompt, y_sample, jnp.stack(conv_p), jnp.stack(h_p), kp, vp, fp,
            jnp.stack(conv_s), jnp.stack(h_s), ks_, vs_, fs_)
```

```python
import os
import numpy as np
import concourse.bass as bass
import concourse.mybir as mybir
from concourse.bass_utils import run_bass_kernel_spmd

F32 = mybir.dt.float32
BF16 = mybir.dt.bfloat16
AF = mybir.ActivationFunctionType
ALU = mybir.AluOpType

D = 2048
DC = 16
DFF = 5632
FC = 44
DR = 2560
RC = 20
NH = 16
HD = 128
SEQ = 4096
TP = 512
DSEQ = 64
NS = 2
PAST = 1024
SKEYS = PAST + DSEQ
ATTN_SCALE = HD ** -0.5
EPS = 1e-6
N_CORES = 8
SEM_EPOCH = 20000
WSLOT = 11264
NWSLOT = 2
A_STOP = int(os.environ.get('A_STOP', '99'))
A_VAR = os.environ.get('A_VAR', '')
KV_STOP = int(os.environ.get('KV_STOP', '99'))
KV_VAR = os.environ.get('KV_VAR', 'dve')


class Buf:
    __slots__ = ("name", "w", "r", "const")

    def __init__(self, name="", const=False):
        self.name = name
        self.w = {}
        self.r = {}
        self.const = const


class _Eng:
    def __init__(self, trk, name, eng, is_pe=False):
        self.trk, self.name, self.eng, self.is_pe = trk, name, eng, is_pe
        self.seen = {}
        self.nsem = 0
        self.new_sem()

    def new_sem(self):
        self.sem = self.trk.nc.alloc_semaphore(f"s_{self.name}_{self.nsem}")
        self.nsem += 1
        self.cnt = 0


class Tracker:
    def __init__(self, nc):
        self.nc = nc
        self.E = {
            "pe": _Eng(self, "pe", nc.tensor, True),
            "act": _Eng(self, "act", nc.scalar),
            "dve": _Eng(self, "dve", nc.vector),
            "pool": _Eng(self, "pool", nc.gpsimd),
            "sp": _Eng(self, "sp", nc.sync),
        }
        self.dma_sems = {}
        self.n_wait = 0
        self.n_ins = 0

    def _deps(self, e, reads, writes):
        need = {}

        def add(k, t, is_waw=False):
            sem, val, src = t
            if src == e.name and (e.is_pe or is_waw):
                return
            if k not in need or need[k][1] < val:
                need[k] = (sem, val)
        for b in reads:
            for k, t in b.w.items():
                add(k, t)
        for b in writes:
            for k, t in b.w.items():
                add(k, t, True)
            for k, t in b.r.items():
                add(k, t)
        for k, (sem, val) in need.items():
            if e.seen.get(k, 0) >= val:
                continue
            e.eng.wait_ge(sem, val)
            e.seen[k] = val
            self.n_wait += 1

    @staticmethod
    def _put(d, t):
        k = t[0].num
        if k not in d or d[k][1] < t[1]:
            d[k] = t

    def _commit(self, t, reads, writes):
        for b in reads:
            if not b.const:
                self._put(b.r, t)
        for b in writes:
            self._put(b.w, t)

    def op(self, ename, fn, reads=(), writes=()):
        e = self.E[ename]
        self._deps(e, reads, writes)
        ins = fn()
        if e.cnt >= SEM_EPOCH:
            e.new_sem()
        ins.then_inc(e.sem, 1)
        e.cnt += 1
        self.n_ins += 1
        self._commit((e.sem, e.cnt, e.name), reads, writes)

    def dma(self, ename, out, in_, reads=(), writes=(), key=None, **kw):
        e = self.E[ename]
        self._deps(e, reads, writes)
        if key is None:
            key = writes[0] if writes else reads[0]
        kk = (ename, id(key))
        ent = self.dma_sems.get(kk)
        if ent is None or ent[1] >= SEM_EPOCH * 16:
            ent = [self.nc.alloc_semaphore(f"d{len(self.dma_sems)}_{self.n_ins}"), 0, key]
            self.dma_sems[kk] = ent
        ins = e.eng.dma_start(out=out, in_=in_, **kw)
        ins.then_inc(ent[0], 16)
        ent[1] += 16
        self.n_ins += 1
        self._commit((ent[0], ent[1], "dma"), reads, writes)

    def wait_all(self, ename, bufs):
        self._deps(self.E[ename], bufs, ())


def build_nc(n_ptiles=SEQ // TP, do_sample=True, stages="x,f0,a,f1,kv,f2,at,f3,out,pre"):
    ST = set(stages.split(","))
    nc = bass.Bass("TRN2", target_bir_lowering=False)
    tk = Tracker(nc)

    def din(name, shape):
        return nc.dram_tensor(name, list(shape), F32, kind="ExternalInput").ap()

    def dout(name, shape):
        return nc.dram_tensor(name, list(shape), F32, kind="ExternalOutput").ap()

    xp = din("xp", [SEQ, D])
    xs = din("xs", [NS * DSEQ, D])
    st_conv = din("st_conv", [NS * 3, DR])
    st_h = din("st_h", [NS, DR])
    cache_k = din("cache_k", [NS, PAST, D])
    cache_v = din("cache_v", [NS, PAST, D])
    cache_lf = din("cache_lf", [NS, PAST, NH])
    ffn_norm = din("ffn_norm", [4, D])
    ffn_w_in = din("ffn_w_in", [4, D, 2 * DFF])
    ffn_w_out = din("ffn_w_out", [4, DFF, D])
    a_norm = din("a_norm", [1, D])
    a_w_in = din("a_w_in", [D, 2 * DR])
    a_conv_w = din("a_conv_w", [4, DR])
    a_conv_b = din("a_conv_b", [1, DR])
    a_gate_w = din("a_gate_w", [2, 10, 256, 256])
    a_gate_b = din("a_gate_b", [2, DR])
    a_lambda = din("a_lambda", [1, DR])
    a_w_out = din("a_w_out", [DR, D])
    kv_norm = din("kv_norm", [1, D])
    w_kv = din("w_kv", [D, 2 * D])
    w_f = din("w_f", [D, NH])
    b_f = din("b_f", [1, NH])
    b_norm = din("b_norm", [1, D])
    b_w_qg = din("b_w_qg", [D, 2 * D])
    b_w_o = din("b_w_o", [D, D])
    final_norm = din("final_norm", [1, D])
    y_p = dout("y_p", [SEQ, D])
    y_s = dout("y_s", [NS * DSEQ, D])
    nconv_p = dout("nconv_p", [3, DR])
    nh_p = dout("nh_p", [1, DR])
    nk_p = dout("nk_p", [SEQ, D])
    nv_p = dout("nv_p", [SEQ, D])
    nlf_p = dout("nlf_p", [SEQ, NH])
    nconv_s = dout("nconv_s", [NS * 3, DR])
    nh_s = dout("nh_s", [NS, DR])
    nk_s = dout("nk_s", [NS * DSEQ, D])
    nv_s = dout("nv_s", [NS * DSEQ, D])
    nlf_s = dout("nlf_s", [NS * DSEQ, NH])
    OUT_BUFS = {}

    def obuf(name):
        if name not in OUT_BUFS:
            OUT_BUFS[name] = Buf(name)
        return OUT_BUFS[name]

    SKP = 1152
    KTp = nc.dram_tensor("KTp", [NH, HD, SEQ], BF16).ap()
    Vp = nc.dram_tensor("Vp", [SEQ, D], BF16).ap()
    KTs = [nc.dram_tensor(f"KTs{s}", [NH, HD, SKP], BF16).ap() for s in range(NS)]
    Vs = [nc.dram_tensor(f"Vs{s}", [SKP, D], BF16).ap() for s in range(NS)]
    B_KTp, B_Vp = Buf("KTp"), Buf("Vp")
    B_KTs = [Buf(f"KTs{s}") for s in range(NS)]
    B_Vs = [Buf(f"Vs{s}") for s in range(NS)]

    def sb(name, shape, dt=F32):
        return nc.alloc_sbuf_tensor(name, list(shape), dt)

    xT = sb("xT", [128, DC, TP]);            B_xT = Buf("xT")
    xn = sb("xn", [128, DC, TP], BF16);      B_xn = Buf("xn")
    tok = [sb(f"tok{i}", [128, D]) for i in range(2)]
    B_tok = [Buf(f"tok{i}") for i in range(2)]
    rstd = sb("rstd", [128, TP]);            B_rstd = Buf("rstd")
    wsl = [sb(f"wsl{i}", [128, WSLOT], BF16) for i in range(NWSLOT)]
    B_wsl = [Buf(f"wsl{i}") for i in range(NWSLOT)]
    gwsl = [sb(f"gwsl{i}", [128, 2, 2, 256], BF16) for i in range(2)]
    B_gwsl = [Buf(f"gwsl{i}") for i in range(2)]
    ARENA = 65536
    arena = sb("arena", [128, ARENA // 4])
    ident = sb("ident", [128, 128]);         B_c = Buf("consts", const=True)
    ones_f = sb("ones_f", [128, 128])
    ones_b = sb("ones_b", [128, 128], BF16)
    tri = sb("tri", [128, 128])
    epsc = sb("epsc", [128, 1])
    onec = sb("onec", [128, 1])
    masks = sb("masks", [128, 4, TP], BF16)
    gn = sb("gn", [128, DC, 8])
    prm = sb("prm", [128, RC, 8])
    nsp8 = sb("nsp8", [128, RC])
    nsp16 = sb("nsp16", [128, RC])
    wfb = sb("wfb", [128, DC, NH], BF16)
    bfb = sb("bfb", [128, NH])
    convst_p = sb("convst_p", [128, RC, 1, 3]); hst_p = sb("hst_p", [128, RC, 1])
    convst_s = sb("convst_s", [128, RC, NS, 3]); hst_s = sb("hst_s", [128, RC, NS])
    B_st_p, B_st_s = Buf("st_p"), Buf("st_s")
    ck_p = sb("ck_p", [128, SEQ // 128, NH]); tot_p = sb("tot_p", [128, NH]); B_ck_p = Buf("ck_p")
    ck_s = [sb(f"ck_s{s}", [128, 9, NH]) for s in range(NS)]
    tot_s = [sb(f"tot_s{s}", [128, NH]) for s in range(NS)]
    B_ck_s = [Buf(f"ck_s{s}") for s in range(NS)]
    biasb = sb("biasb", [128, SEQ // 128, NH]); B_bias = Buf("biasb")
    lf = sb("lf", [128, 4, NH]);              B_lf = Buf("lf")
    lft = sb("lft", [128, NH]);               B_lft = Buf("lft")
    sqs = [sb(f"sq{i}", [128, TP]) for i in range(2)]
    B_sq = [Buf(f"sq{i}") for i in range(2)]
    cst = sb("cst", [128, 64]);               B_cst = Buf("cst")
    zT = sb("zT", [NH, TP]);                  B_zT = Buf("zT")
    stsm = sb("stsm", [64, 128]);             B_stsm = Buf("stsm")

    ps = [nc.alloc_psum_tensor(f"ps{i}", [128, 512], F32) for i in range(8)]
    B_ps = [Buf(f"ps{i}") for i in range(8)]
    rr = {"bank": 0, "w": 0, "gw": 0, "tok": 0, "sq": 0}

    def nbank():
        i = rr["bank"]
        rr["bank"] = (i + 1) % 6
        return ps[i], B_ps[i]

    def carve(name, off_bytes, shape, dt):
        esz = 4 if dt == F32 else 2
        n = int(np.prod(shape[1:]))
        v = arena[:, off_bytes // 4: off_bytes // 4 + (n * esz + 3) // 4]
        if dt != F32:
            v = v.bitcast(dt)
        if len(shape) > 2:
            letters = "abcdefg"[: len(shape) - 1]
            pat = "p (" + " ".join(letters) + ") -> p " + " ".join(letters)
            v = v[:, 0:n].rearrange(pat, **{l: s for l, s in zip(letters, shape[1:])})
        else:
            v = v[:, 0:n]
        return v

    B_arena = Buf("arena")
    arena_users = []

    def phase_bufs(names):
        bufs = [Buf(n) for n in names]
        for b in bufs:
            for old in arena_users:
                for k, t in old.w.items():
                    Tracker._put(b.w, (t[0], t[1], "fence"))
                for k, t in old.r.items():
                    Tracker._put(b.w, (t[0], t[1], "fence"))
        arena_users.extend(bufs)
        if len(arena_users) > 64:
            agg = Buf("agg")
            for old in arena_users:
                for k, t in list(old.w.items()) + list(old.r.items()):
                    Tracker._put(agg.w, (t[0], t[1], "fence"))
            arena_users[:] = [agg] + bufs
        return bufs

    def act(out, in_, func, R, W, **kw):
        tk.op("act", lambda: nc.scalar.activation(out=out, in_=in_, func=func, **kw), R, W)

    def vcopy(out, in_, R, W, eng="dve"):
        if eng == "dve":
            tk.op("dve", lambda: nc.vector.tensor_copy(out=out, in_=in_), R, W)
        else:
            tk.op("act", lambda: nc.scalar.copy(out=out, in_=in_), R, W)

    def vtt(out, a, b, op, R, W):
        tk.op("dve", lambda: nc.vector.tensor_tensor(out=out, in0=a, in1=b, op=op), R, W)

    def vts(out, a, s1, s2, op0, op1, R, W):
        if s2 is None:
            tk.op("dve", lambda: nc.vector.tensor_scalar(out=out, in0=a, scalar1=s1, scalar2=None, op0=op0), R, W)
        else:
            tk.op("dve", lambda: nc.vector.tensor_scalar(out=out, in0=a, scalar1=s1, scalar2=s2, op0=op0, op1=op1), R, W)

    def vstt(out, a, s, b, op0, op1, R, W):
        tk.op("dve", lambda: nc.vector.scalar_tensor_tensor(out=out, in0=a, scalar=s, in1=b, op0=op0, op1=op1), R, W)

    def mmg(out, pairs, R, W):
        def f():
            n = len(pairs)
            for i, (l, r) in enumerate(pairs):
                last = nc.tensor.matmul(out, lhsT=l, rhs=r, start=(i == 0), stop=(i == n - 1))
            return last
        tk.op("pe", f, R, W)

    def wload(parts, kc, ncols):
        i = rr["w"]
        rr["w"] = (i + 1) % NWSLOT
        npart = len(parts)
        v = wsl[i][:, 0: kc * npart * ncols].rearrange("p (c g f) -> p c g f", c=kc, g=npart)
        for g, (W2, c0) in enumerate(parts):
            src = W2.rearrange("(c p) f -> p c f", p=128)[:, :, c0:c0 + ncols]
            tk.dma("pool", v[:, :, g, :], src, writes=[B_wsl[i]])
        return v, B_wsl[i]

    rowst = arena[0:8, 0:DR]
    maskf = arena[:, DR:DR + TP]
    (B_rows, B_mk) = phase_bufs(["rows", "maskf"])
    tk.op("pool", lambda: nc.gpsimd.memset(ones_f[:], 1.0), writes=[B_c])
    tk.op("pool", lambda: nc.gpsimd.memset(ones_b[:], 1.0), writes=[B_c])
    tk.op("pool", lambda: nc.gpsimd.memset(epsc[:], EPS), writes=[B_c])
    tk.op("pool", lambda: nc.gpsimd.memset(onec[:], 1.0), writes=[B_c])
    tk.op("pool", lambda: nc.gpsimd.memset(ident[:], 0.0), writes=[B_c])
    tk.op("pool", lambda: nc.gpsimd.affine_select(out=ident[:], in_=ident[:], pattern=[[-1, 128]], compare_op=ALU.not_equal,
                                                   fill=1.0, base=0, channel_multiplier=1), [B_c], [B_c])
    tk.op("pool", lambda: nc.gpsimd.memset(tri[:], 1.0), writes=[B_c])
    tk.op("pool", lambda: nc.gpsimd.affine_select(out=tri[:], in_=tri[:], pattern=[[1, 128]], compare_op=ALU.is_ge,
                                                   fill=0.0, base=0, channel_multiplier=-1), [B_c], [B_c])
    for kb in range(4):
        tk.op("pool", lambda: nc.gpsimd.memset(maskf[:], 1.0), [B_mk], [B_mk])
        tk.op("pool", lambda kb=kb: nc.gpsimd.affine_select(out=maskf[:], in_=maskf[:], pattern=[[1, TP]], compare_op=ALU.is_ge,
                                                             fill=0.0, base=-128 * kb, channel_multiplier=-1), [B_mk], [B_mk])
        tk.op("pool", lambda kb=kb: nc.gpsimd.tensor_copy(out=masks[:, kb, :], in_=maskf[:]), [B_mk], [B_c])
    tk.op("pool", lambda: nc.gpsimd.memset(convst_p[:], 0.0), writes=[B_st_p])
    tk.op("pool", lambda: nc.gpsimd.memset(hst_p[:], 0.0), writes=[B_st_p])
    tk.op("pool", lambda: nc.gpsimd.memset(tot_p[:], 0.0), writes=[B_ck_p])
    tk.op("pool", lambda: nc.gpsimd.memset(lf[:], 0.0), writes=[B_lf])
    tk.op("pool", lambda: nc.gpsimd.memset(bfb[:], 0.0), writes=[B_c])
    for s in range(NS):
        tk.op("pool", lambda s=s: nc.gpsimd.memset(tot_s[s][:], 0.0), writes=[B_ck_s[s]])


    def rows_to_cols(nrows, nchunks, dst3, dstB):
        bank, Bb = nbank()

        def f():
            for c in range(nchunks):
                last = nc.tensor.transpose(out=bank[:, c * nrows:(c + 1) * nrows], in_=rowst[0:nrows, c * 128:(c + 1) * 128],
                                           identity=ident[0:nrows, 0:nrows])
            return last
        tk.op("pe", f, [B_rows, B_c], [Bb])
        vcopy(dst3, bank[:, 0:nchunks * nrows].rearrange("p (c r) -> p c r", c=nchunks), [Bb], [dstB])

    tk.dma("sp", rowst[0:4, 0:D], ffn_norm, writes=[B_rows])
    for i, a in enumerate([a_norm, kv_norm, b_norm, final_norm]):
        tk.dma("sp", rowst[4 + i:5 + i, 0:D], a, writes=[B_rows])
    rows_to_cols(8, DC, gn[:], B_c)
    tk.dma("sp", rowst[0:4, :], a_conv_w, writes=[B_rows])
    tk.dma("sp", rowst[4:5, :], a_conv_b, writes=[B_rows])
    tk.dma("sp", rowst[5:7, :], a_gate_b, writes=[B_rows])
    tk.dma("sp", rowst[7:8, :], a_lambda, writes=[B_rows])
    rows_to_cols(8, RC, prm[:], B_c)
    act(nsp8[:], prm[:, :, 7], AF.Exp, [B_c], [B_c], scale=-1.0)
    act(nsp8[:], nsp8[:], AF.Ln, [B_c], [B_c], bias=onec[:], scale=1.0)
    vts(nsp16[:], nsp8[:], -16.0, None, ALU.mult, None, [B_c], [B_c])
    vts(nsp8[:], nsp8[:], -8.0, None, ALU.mult, None, [B_c], [B_c])
    tk.dma("pool", wfb[:], w_f.rearrange("(c p) h -> p c h", p=128), writes=[B_c])
    tk.dma("sp", bfb[0:1, :], b_f, reads=[B_c], writes=[B_c])
    bank, Bb = nbank()
    mmg(bank[:, 0:NH], [(ones_f[:], bfb[:])], [B_c], [Bb])
    vcopy(bfb[:], bank[:, 0:NH], [Bb, B_c], [B_c])

    def rms(T, gidx, dst, dstB, dst_f32=False):
        bank, Bb = nbank()
        for dc in range(DC):
            i = rr["sq"]
            rr["sq"] = (i + 1) % 2
            act(sqs[i][:, :T], xT[:, dc, :T], AF.Square, [B_xT], [B_sq[i]])
            tk.op("pe", lambda i=i, dc=dc: nc.tensor.matmul(bank[:, :T], lhsT=ones_f[:], rhs=sqs[i][:, :T], start=(dc == 0),
                                                             stop=(dc == DC - 1)), [B_sq[i], B_c], [Bb])
        act(rstd[:, :T], bank[:, :T], AF.Sqrt, [Bb, B_c], [B_rstd], scale=1.0 / D, bias=epsc[:])
        tk.op("dve", lambda: nc.vector.reciprocal(out=rstd[:, :T], in_=rstd[:, :T]), [B_rstd], [B_rstd])
        for dc in range(DC):
            vstt(dst[:, dc, :T], xT[:, dc, :T], gn[:, dc, gidx:gidx + 1], rstd[:, :T], ALU.mult, ALU.mult,
                 [B_xT, B_rstd, B_c], [dstB])

    def ffn(T, idx):
        (B_hid, B_sg0, B_sg1) = phase_bufs(["hid", "sg0", "sg1"])
        hid = carve("hid", 0, [128, FC, TP], BF16)
        sg = [carve("sg0", 45056, [128, TP], F32), carve("sg1", 45056 + 2048, [128, TP], F32)]
        B_sg = [B_sg0, B_sg1]
        rms(T, idx, xn, B_xn)
        Win, Wout = ffn_w_in[idx], ffn_w_out[idx]
        for pj in range(FC // 2):
            wv, Bw = wload([(Win, 256 * pj), (Win, DFF + 256 * pj)], DC, 256)
            for jj in range(2):
                j = 2 * pj + jj
                bg, Bg = nbank()
                mmg(bg[:, :T], [(wv[:, dc, 0, jj * 128:(jj + 1) * 128], xn[:, dc, :T]) for dc in range(DC)], [Bw, B_xn], [Bg])
                bu, Bu = nbank()
                mmg(bu[:, :T], [(wv[:, dc, 1, jj * 128:(jj + 1) * 128], xn[:, dc, :T]) for dc in range(DC)], [Bw, B_xn], [Bu])
                act(sg[jj][:, :T], bg[:, :T], AF.Silu, [Bg], [B_sg[jj]])
                vtt(hid[:, j, :T], sg[jj][:, :T], bu[:, :T], ALU.mult, [B_sg[jj], Bu], [B_hid])
        for pd in range(8):
            wv, Bw = wload([(Wout, 256 * pd)], FC, 256)
            for dd in range(2):
                dc = 2 * pd + dd
                bo, Bo = nbank()
                mmg(bo[:, :T], [(wv[:, j, 0, dd * 128:(dd + 1) * 128], hid[:, j, :T]) for j in range(FC)], [Bw, B_hid], [Bo])
                vstt(xT[:, dc, :T], bo[:, :T], 0.5, xT[:, dc, :T], ALU.mult, ALU.add, [Bo, B_xT], [B_xT])

    def out_rows(src3, nchunks, nrows, srcB, dst_ap, dstB):
        n = nchunks * nrows
        if len(src3.shape) == 3:
            vcopy(cst[:, 0:n].rearrange("p (c r) -> p c r", c=nchunks), src3, [srcB], [B_cst])
        else:
            vcopy(cst[:, 0:n], src3, [srcB], [B_cst])
        bank, Bb = nbank()
        tk.op("pe", lambda: nc.tensor.transpose(out=bank[0:n, 0:128], in_=cst[:, 0:n], identity=ident[:]), [B_cst, B_c], [Bb])
        vcopy(stsm[0:n, :], bank[0:n, 0:128], [Bb], [B_stsm])
        for c in range(nchunks):
            tk.dma("sp", dst_ap[:, c * 128:(c + 1) * 128], stsm[c * nrows:(c + 1) * nrows, :], reads=[B_stsm], writes=[dstB])

    def alayer(T, segs, convst, hst, B_st, last, conv_out, h_out):
        nseg = len(segs)
        L = segs[0][1]
        names = ["gt", "ub", "uc", "ucb", "r", "i", "a", "a2", "h", "t", "hg"]
        Bd = dict(zip(names, phase_bufs(names)))
        off = 0

        def cv(name, shape, dt):
            nonlocal off
            v = carve(name, off, shape, dt)
            esz = 4 if dt == F32 else 2
            off += ((int(np.prod(shape[1:])) * esz + 63) // 64) * 64
            return v
        gt = cv("gt", [128, 2, T], F32)
        ub = cv("ub", [128, 2, nseg, L + 3], F32)
        uc = cv("uc", [128, 2, nseg, L], F32)
        ucb = cv("ucb", [128, 2, nseg, L], BF16)
        rT = cv("r", [128, 2, nseg, L], F32)
        iT = cv("i", [128, 2, nseg, L], F32)
        aT = cv("a", [128, 2, nseg, L], F32)
        a2 = cv("a2", [128, 2, nseg, L], F32)
        hT = cv("h", [128, 2, nseg, L], F32)
        tT = cv("t", [128, 2, T], F32)
        hg = cv("hg", [128, RC, TP], BF16)
        assert off <= ARENA, off
        rms(T, 4, xn, B_xn)
        for hb in range(10):
            wv, Bw = wload([(a_w_in, 256 * hb), (a_w_in, DR + 256 * hb)], DC, 256)
            gi = rr["gw"]
            rr["gw"] = 1 - gi
            for g in range(2):
                tk.dma("pool", gwsl[gi][:, g], a_gate_w[g, hb].rearrange("(ic p) j -> p ic j", p=128), writes=[B_gwsl[gi]])
            gw = gwsl[gi]
            for cc in range(2):
                c = 2 * hb + cc
                bg, Bg = nbank()
                mmg(bg[:, :T], [(wv[:, dc, 0, cc * 128:(cc + 1) * 128], xn[:, dc, :T]) for dc in range(DC)], [Bw, B_xn], [Bg])
                vcopy(gt[:, cc, :], bg[:, :T], [Bg], [Bd["gt"]], eng="act")
                bu, Bu = nbank()
                mmg(bu[:, :T], [(wv[:, dc, 1, cc * 128:(cc + 1) * 128], xn[:, dc, :T]) for dc in range(DC)], [Bw, B_xn], [Bu])
                vcopy(ub[:, cc, :, 3:3 + L], bu[:, :T].rearrange("p (s l) -> p s l", s=nseg), [Bu], [Bd["ub"]], eng="act")
                if A_STOP < 2:
                    continue
                vcopy(ub[:, cc, :, 0:3], convst[:, c, :, :], [B_st], [Bd["ub"]])
                vts(uc[:, cc], ub[:, cc, :, 3:3 + L], prm[:, c, 3:4], prm[:, c, 4:5], ALU.mult, ALU.add, [Bd["ub"], B_c], [Bd["uc"]])
                for k in range(3):
                    vstt(uc[:, cc], ub[:, cc, :, k:k + L], prm[:, c, k:k + 1], uc[:, cc], ALU.mult, ALU.add,
                         [Bd["ub"], Bd["uc"], B_c], [Bd["uc"]])
                vcopy(convst[:, c, :, :], ub[:, cc, :, L:L + 3], [Bd["ub"]], [B_st])
                vcopy(ucb[:, cc], uc[:, cc], [Bd["uc"]], [Bd["ucb"]])
            if A_STOP < 3:
                continue
            for g, dstT, nm in ((0, rT, "r"), (1, iT, "i")):
                for jc in range(2):
                    bq, Bq = nbank()
                    mmg(bq[:, :T], [(gw[:, g, ic, jc * 128:(jc + 1) * 128], ucb[:, ic].rearrange("p s l -> p (s l)"))
                                    for ic in range(2)], [B_gwsl[gi], Bd["ucb"]], [Bq])
                    act(dstT[:, jc].rearrange("p s l -> p (s l)"), bq[:, :T], AF.Sigmoid, [Bq, B_c], [Bd[nm]],
                        bias=prm[:, 2 * hb + jc, 5 + g:6 + g], scale=1.0)
            for cc in range(2):
                if A_STOP < 4:
                    continue
                c = 2 * hb + cc
                fl = lambda v: v[:, cc].rearrange("p s l -> p (s l)")
                act(fl(aT), fl(rT), AF.Exp, [Bd["r"], B_c], [Bd["a"]], scale=nsp8[:, c:c + 1])
                act(fl(a2), fl(rT), AF.Exp, [Bd["r"], B_c], [Bd["a2"]], scale=nsp16[:, c:c + 1])
                act(fl(a2), fl(a2), AF.Sqrt, [Bd["a2"], B_c], [Bd["a2"]], scale=-1.0, bias=onec[:])
                vtt(fl(iT), fl(iT), fl(uc), ALU.mult, [Bd["i"], Bd["uc"]], [Bd["i"]])
                vtt(fl(iT), fl(iT), fl(a2), ALU.mult, [Bd["i"], Bd["a2"]], [Bd["i"]])
                for s in range(nseg):
                    if A_STOP < 5:
                        continue
                    tk.op("dve", lambda s=s, cc=cc, c=c: nc.vector.tensor_tensor_scan(
                        out=hT[:, cc, s, :], data0=aT[:, cc, s, :], data1=iT[:, cc, s, :], initial=hst[:, c, s:s + 1],
                        op0=ALU.mult, op1=ALU.add), [Bd["a"], Bd["i"], B_st], [Bd["h"]])
                    vcopy(hst[:, c, s:s + 1], hT[:, cc, s, L - 1:L], [Bd["h"]], [B_st])
                if A_STOP < 6:
                    continue
                if A_VAR != "onlyhg":
                    act(tT[:, cc, :], gt[:, cc, :], AF.Square, [Bd["gt"]], [Bd["t"]], scale=0.044715 ** 0.5)
                    vstt(tT[:, cc, :], tT[:, cc, :], 1.0, gt[:, cc, :], ALU.add, ALU.mult, [Bd["t"], Bd["gt"]], [Bd["t"]])
                    act(tT[:, cc, :], tT[:, cc, :], AF.Sigmoid, [Bd["t"]], [Bd["t"]], scale=1.5957691216057308)
                    vtt(tT[:, cc, :], tT[:, cc, :], gt[:, cc, :], ALU.mult, [Bd["t"], Bd["gt"]], [Bd["t"]])
                if A_VAR != "nohg":
                    vtt(hg[:, c, :T], tT[:, cc, :], fl(hT), ALU.mult, [Bd["t"], Bd["h"]], [Bd["hg"]])
        for pd in range(4 if A_STOP >= 7 else 0):
            wv, Bw = wload([(a_w_out, 512 * pd)], RC, 512)
            for dd in range(4):
                dc = 4 * pd + dd
                bo, Bo = nbank()
                mmg(bo[:, :T], [(wv[:, j, 0, dd * 128:(dd + 1) * 128], hg[:, j, :T]) for j in range(RC)], [Bw, Bd["hg"]], [Bo])
                vtt(xT[:, dc, :T], bo[:, :T], xT[:, dc, :T], ALU.add, [Bo, B_xT], [B_xT])
        if last and A_STOP >= 8:
            for s in range(nseg):
                for half in range(2):
                    c0 = 10 * half
                    out_rows(convst[:, c0:c0 + 10, s, :], 10, 3, B_st, conv_out[3 * s:3 * s + 3, c0 * 128:(c0 + 10) * 128],
                             obuf("conv"))
                out_rows(hst[:, :, s], RC, 1, B_st, h_out[s:s + 1, :], obuf("h"))

    def cumsum_block(lfblk, ck_dst, tot, B_ck, trimat=None):
        b1, B1 = nbank()
        mmg(b1[:, 0:NH], [(tri[:], lfblk)], [B_lf, B_c], [B1])
        b2, B2 = nbank()
        mmg(b2[:, 0:NH], [(ones_f[:], lfblk)], [B_lf, B_c], [B2])
        vtt(ck_dst, b1[:, 0:NH], tot[:], ALU.add, [B1, B_ck], [B_ck])
        vtt(tot[:], b2[:, 0:NH], tot[:], ALU.add, [B2, B_ck], [B_ck])

    def kvphase(T, tokblocks, lfblocks, k_out, v_out, lf_out, kt_dsts, v_dsts):
        (B_kst, B_vbf, B_ktst) = phase_bufs(["kst", "vbf", "ktst"])
        nb = len(tokblocks)
        kfT = carve("kst", 0, [128, NH, TP], F32)
        vbf = carve("vbf", 32768, [128, 4, D], BF16)
        ktst = carve("ktst", 49152, [128, NH, TP], BF16)
        rms(T, 5, xn, B_xn)
        for half in range(2):
            for p in range(4):
                wv, Bw = wload([(w_kv, half * D + 512 * p)], DC, 512)
                for hh in range(4):
                    h = 4 * p + hh
                    bo, Bo = nbank()
                    mmg(bo[:, :T], [(wv[:, dc, 0, hh * 128:(hh + 1) * 128], xn[:, dc, :T]) for dc in range(DC)], [Bw, B_xn], [Bo])
                    vcopy(kfT[:, h, :T], bo[:, :T], [Bo], [B_kst], eng="dve" if KV_VAR == "dve" else "act")
                    if half == 0:
                        vcopy(ktst[:, h, :T], bo[:, :T], [Bo], [B_ktst])
            dst = k_out if half == 0 else v_out
            for tb, (c0, n) in enumerate(tokblocks):
                i = rr["tok"]
                rr["tok"] = 1 - i
                for g in range(4):
                    bank, Bb = nbank()

                    def f(g=g, c0=c0, n=n, bank=bank):
                        for j in range(4):
                            last = nc.tensor.transpose(out=bank[0:n, j * 128:(j + 1) * 128], in_=kfT[:, 4 * g + j, c0:c0 + n],
                                                       identity=ident[:])
                        return last
                    tk.op("pe", f, [B_kst, B_c], [Bb])
                    vcopy(tok[i][0:n, 512 * g:512 * (g + 1)], bank[0:n, :], [Bb], [B_tok[i]], eng="dve" if KV_VAR == "dve" else "act")
                    if half == 1:
                        vcopy(vbf[0:n, tb, 512 * g:512 * (g + 1)], bank[0:n, :], [Bb], [B_vbf])
                if KV_STOP >= 2:
                    tk.dma("sp", dst[c0:c0 + n, :], tok[i][0:n, :], reads=[B_tok[i]], writes=[obuf("k" if half == 0 else "v")])
        if KV_STOP < 3:
            return
        for (dram, Bd_, col0, ncol, k0) in kt_dsts:
            tk.dma("sp", dram[:, :, k0:k0 + ncol].rearrange("h p k -> p h k"), ktst[:, :, col0:col0 + ncol], reads=[B_ktst],
                   writes=[Bd_])
        for (dram, Bd_, tb, r0, nrow, k0) in v_dsts:
            tk.dma("sp", dram[k0:k0 + nrow, :], vbf[r0:r0 + nrow, tb, :], reads=[B_vbf], writes=[Bd_])
        for bi, (c0, n, ck_dst, tot, B_ck, lf_rows) in enumerate(lfblocks if KV_STOP >= 4 else []):
            if bi == 0:
                bz, Bz = nbank()
                mmg(bz[0:NH, :T], [(wfb[:, dc, :], xn[:, dc, :T]) for dc in range(DC)], [B_xn, B_c], [Bz])
                vcopy(zT[0:NH, :T], bz[0:NH, :T], [Bz], [B_zT], eng="act")
            bo, Bo = nbank()
            tk.op("pe", lambda bo=bo, c0=c0, n=n: nc.tensor.transpose(out=bo[0:n, 0:NH], in_=zT[0:NH, c0:c0 + n], identity=ident[0:NH, 0:NH]),
                  [B_zT, B_c], [Bo])
            vtt(lft[0:n, :], bo[0:n, 0:NH], bfb[0:n, :], ALU.add, [Bo, B_c], [B_lft])
            act(lft[0:n, :], lft[0:n, :], AF.Exp, [B_lft], [B_lft], scale=-1.0)
            act(lft[0:n, :], lft[0:n, :], AF.Ln, [B_lft, B_c], [B_lft], bias=onec[0:n, :], scale=1.0)
            vts(lf[0:n, bi, :], lft[0:n, :], -1.0, None, ALU.mult, None, [B_lft], [B_lf])
            tk.dma("sp", lf_rows, lf[0:n, bi, :], reads=[B_lf], writes=[obuf("lf")])
            if KV_STOP >= 5:
                cumsum_block(lf[:, bi, :], ck_dst, tot, B_ck)

    def attention(T, qsegs):
        names = ["qT", "sg", "kt0", "kt1", "vv0", "vv1", "pT0", "pT1", "pT2", "rden", "otmp"]
        Bd = dict(zip(names, phase_bufs(names)))
        qT = carve("qT", 0, [128, NH, TP], BF16)
        sg = carve("sg", 16384, [128, NH, TP], F32)
        kts = [carve("kt", 49152 + 1024 * i, [128, 512], BF16) for i in range(2)]
        vvs = [carve("vv", 51200 + 1024 * i, [128, 4, 128], BF16) for i in range(2)]
        pTs = [carve("pT", 53248 + 1024 * i, [128, 512], BF16) for i in range(3)]
        rden = carve("rden", 56320, [128, 512], F32)
        otmp = carve("otmp", 58368, [128, 512], F32)
        rms(T, 6, xn, B_xn)
        for p in range(4):
            wv, Bw = wload([(b_w_qg, 512 * p)], DC, 512)
            for hh in range(4):
                h = 4 * p + hh
                bo, Bo = nbank()
                mmg(bo[:, :T], [(wv[:, dc, 0, hh * 128:(hh + 1) * 128], xn[:, dc, :T]) for dc in range(DC)], [Bw, B_xn], [Bo])
                vcopy(qT[:, h, :T], bo[:, :T], [Bo], [Bd["qT"]])
        for p in range(4):
            wv, Bw = wload([(b_w_qg, D + 512 * p)], DC, 512)
            for hh in range(4):
                h = 4 * p + hh
                bo, Bo = nbank()
                mmg(bo[:, :T], [(wv[:, dc, 0, hh * 128:(hh + 1) * 128], xn[:, dc, :T]) for dc in range(DC)], [Bw, B_xn], [Bo])
                act(sg[:, h, :T], bo[:, :T], AF.Sigmoid, [Bo], [Bd["sg"]])
        og = xn
        cnt = {"kv": 0, "p": 0}
        for h in range(NH):
            for qs in qsegs:
                q0, L = qs["q0"], qs["L"]
                first = True
                nblk_total = sum((n + 127) // 128 for (_, n, _, _) in qs["chunks"])
                done = 0
                for (k0, n, ckblk0, mbase) in qs["chunks"]:
                    si = cnt["kv"] % 2
                    cnt["kv"] += 1
                    nbk = (n + 127) // 128
                    tk.dma("sp", kts[si][:, 0:n], qs["KT"][h, :, k0:k0 + n], reads=[qs["B_KT"]], writes=[Bd[f"kt{si}"]])
                    if n % 128 == 0:
                        tk.dma("sp", vvs[si][:, 0:nbk, :],
                               qs["V"][k0:k0 + n, h * HD:(h + 1) * HD].rearrange("(b p) d -> p b d", p=128),
                               reads=[qs["B_V"]], writes=[Bd[f"vv{si}"]])
                    else:
                        tk.dma("sp", vvs[si][0:n, 0, :], qs["V"][k0:k0 + n, h * HD:(h + 1) * HD],
                               reads=[qs["B_V"]], writes=[Bd[f"vv{si}"]])
                    for kb in range(nbk):
                        nk = min(128, n - 128 * kb)
                        bs, Bs = nbank()
                        mmg(bs[0:nk, 0:L], [(kts[si][:, kb * 128:kb * 128 + nk], qT[:, h, q0:q0 + L])], [Bd[f"kt{si}"], Bd["qT"]], [Bs])
                        pi = cnt["p"] % 3
                        cnt["p"] += 1
                        act(pTs[pi][0:nk, 0:L], bs[0:nk, 0:L], AF.Exp, [Bs, qs["B_bias"]], [Bd[f"pT{pi}"]],
                            bias=qs["bias"](ckblk0 + kb)[0:nk, h:h + 1], scale=ATTN_SCALE)
                        if mbase is not None:
                            vtt(pTs[pi][0:nk, 0:L], pTs[pi][0:nk, 0:L], masks[0:nk, mbase + kb, 0:L], ALU.mult,
                                [Bd[f"pT{pi}"], B_c], [Bd[f"pT{pi}"]])
                        done += 1
                        lastb = done == nblk_total

                        def f(si=si, kb=kb, nk=nk, pi=pi, first=first, lastb=lastb):
                            nc.tensor.matmul(ps[6][:, q0:q0 + L], lhsT=vvs[si][0:nk, kb, :], rhs=pTs[pi][0:nk, 0:L], start=first, stop=lastb)
                            return nc.tensor.matmul(ps[7][:, q0:q0 + L], lhsT=ones_b[0:nk, :], rhs=pTs[pi][0:nk, 0:L], start=first,
                                                    stop=lastb)
                        tk.op("pe", f, [Bd[f"vv{si}"], Bd[f"pT{pi}"], B_c], [B_ps[6], B_ps[7]])
                        first = False
                tk.op("dve", lambda: nc.vector.reciprocal(out=rden[:, 0:L], in_=ps[7][:, q0:q0 + L]), [B_ps[7]], [Bd["rden"]])
                vtt(otmp[:, 0:L], ps[6][:, q0:q0 + L], rden[:, 0:L], ALU.mult, [B_ps[6], Bd["rden"]], [Bd["otmp"]])
                vtt(og[:, h, q0:q0 + L], otmp[:, 0:L], sg[:, h, q0:q0 + L], ALU.mult, [Bd["otmp"], Bd["sg"]], [B_xn])
        for pd in range(4):
            wv, Bw = wload([(b_w_o, 512 * pd)], DC, 512)
            for dd in range(4):
                dc = 4 * pd + dd
                bo, Bo = nbank()
                mmg(bo[:, :T], [(wv[:, j, 0, dd * 128:(dd + 1) * 128], og[:, j, :T]) for j in range(DC)], [Bw, B_xn], [Bo])
                vtt(xT[:, dc, :T], bo[:, :T], xT[:, dc, :T], ALU.add, [Bo, B_xT], [B_xT])

    def load_x(T, src_blocks):
        for tb, src in enumerate(src_blocks):
            i = rr["tok"]
            rr["tok"] = 1 - i
            tk.dma("sp", tok[i][:], src, writes=[B_tok[i]])
            for g in range(4):
                bank, Bb = nbank()

                def f(i=i, g=g, bank=bank):
                    for j in range(4):
                        last = nc.tensor.transpose(out=bank[:, j * 128:(j + 1) * 128], in_=tok[i][:, (4 * g + j) * 128:(4 * g + j + 1) * 128],
                                                   identity=ident[:])
                    return last
                tk.op("pe", f, [B_tok[i], B_c], [Bb])
                vcopy(xT[:, 4 * g:4 * g + 4, tb * 128:(tb + 1) * 128], bank[:, :].rearrange("p (j t) -> p j t", j=4), [Bb], [B_xT],
                      eng="act" if g % 2 else "dve")

    def final_out(T, dst_blocks):
        (B_yT,) = phase_bufs(["yT"])
        yT = carve("yT", 0, [128, DC, TP], F32)
        rms(T, 7, yT, B_yT)
        for tb, dst in enumerate(dst_blocks):
            i = rr["tok"]
            rr["tok"] = 1 - i
            for g in range(4):
                bank, Bb = nbank()

                def f(g=g, tb=tb, bank=bank):
                    for j in range(4):
                        last = nc.tensor.transpose(out=bank[:, j * 128:(j + 1) * 128], in_=yT[:, 4 * g + j, tb * 128:(tb + 1) * 128],
                                                   identity=ident[:])
                    return last
                tk.op("pe", f, [B_yT, B_c], [Bb])
                vcopy(tok[i][:, 512 * g:512 * (g + 1)], bank[:, :], [Bb], [B_tok[i]], eng="act" if g % 2 else "dve")
            tk.dma("sp", dst, tok[i][:], reads=[B_tok[i]], writes=[obuf("y")])

    if do_sample and "pre" in ST:
        tk.dma("sp", rowst[0:6, :], st_conv, reads=[B_rows], writes=[B_rows])
        rows_to_cols(6, RC, convst_s[:].rearrange("p c s k -> p c (s k)"), B_st_s)
        tk.dma("sp", rowst[0:2, :], st_h, reads=[B_rows], writes=[B_rows])
        rows_to_cols(2, RC, hst_s[:], B_st_s)
        for s in range(NS):
            (B_ktc, B_vc, B_lfc) = phase_bufs(["ktc", "vc", "lfc"])
            ktc = carve("ktc", 0, [128, NH, PAST], BF16)
            vc = carve("vc", 32768, [128, 8, D], BF16)
            tk.dma("pool", vc[:], cache_v[s].rearrange("(b p) d -> p b d", p=128), writes=[B_vc])
            tk.dma("sp", Vs[s][0:PAST, :].rearrange("(b p) d -> p b d", p=128), vc[:], reads=[B_vc], writes=[B_Vs[s]])
            for blk in range(8):
                i = rr["tok"]
                rr["tok"] = 1 - i
                tk.dma("sp", tok[i][:], cache_k[s, blk * 128:(blk + 1) * 128, :], writes=[B_tok[i]])
                for g in range(4):
                    bank, Bb = nbank()

                    def f(i=i, g=g, bank=bank):
                        for j in range(4):
                            last = nc.tensor.transpose(out=bank[:, j * 128:(j + 1) * 128],
                                                       in_=tok[i][:, (4 * g + j) * 128:(4 * g + j + 1) * 128], identity=ident[:])
                        return last
                    tk.op("pe", f, [B_tok[i], B_c], [Bb])
                    vcopy(ktc[:, 4 * g:4 * g + 4, blk * 128:(blk + 1) * 128], bank[:, :].rearrange("p (j t) -> p j t", j=4), [Bb],
                          [B_ktc], eng="act" if g % 2 else "dve")
            tk.dma("sp", KTs[s][:, :, 0:PAST].rearrange("h p k -> p h k"), ktc[:], reads=[B_ktc], writes=[B_KTs[s]])
            lfc = ck_s[s]
            tk.dma("sp", lfc[:, 0:8, :], cache_lf[s].rearrange("(b p) h -> p b h", p=128), writes=[B_ck_s[s]])
            for blk in range(8):
                b1, B1 = nbank()
                mmg(b1[:, 0:NH], [(tri[:], lfc[:, blk, :])], [B_ck_s[s], B_c], [B1])
                b2, B2 = nbank()
                mmg(b2[:, 0:NH], [(ones_f[:], lfc[:, blk, :])], [B_ck_s[s], B_c], [B2])
                vtt(lfc[:, blk, :], b1[:, 0:NH], tot_s[s][:], ALU.add, [B1, B_ck_s[s]], [B_ck_s[s]])
                vtt(tot_s[s][:], b2[:, 0:NH], tot_s[s][:], ALU.add, [B2, B_ck_s[s]], [B_ck_s[s]])

    def bias_update(nblk, ck, tot, B_ck):
        for kb in range(nblk):
            vtt(biasb[:, kb, :], tot[:], ck[:, kb, :], ALU.subtract, [B_ck], [B_bias])

    tiles = [("p", i) for i in range(n_ptiles)]
    if do_sample:
        tiles.insert(min(1, len(tiles)), ("s", 0))
    for kind, ti in tiles:
        if kind == "p":
            T = TP
            t0 = ti * TP
            load_x(T, [xp[t0 + 128 * b: t0 + 128 * (b + 1), :] for b in range(4)])
            if "f0" in ST:
                ffn(T, 0)
            if "a" in ST:
                alayer(T, [(0, TP)], convst_p, hst_p, B_st_p, ti == n_ptiles - 1, nconv_p, nh_p)
            if "f1" in ST:
                ffn(T, 1)
            if "kv" in ST:
              kvphase(T, [(128 * b, 128) for b in range(4)],
                    [(128 * b, 128, ck_p[:, 4 * ti + b, :], tot_p, B_ck_p, nlf_p[t0 + 128 * b:t0 + 128 * (b + 1), :]) for b in range(4)],
                    nk_p[t0:t0 + TP, :], nv_p[t0:t0 + TP, :], None,
                    [(KTp, B_KTp, 0, TP, t0)],
                    [(Vp, B_Vp, b, 0, 128, t0 + 128 * b) for b in range(4)])
            if "f2" in ST:
                ffn(T, 2)
            nblk = 4 * (ti + 1)
            if "at" in ST:
                bias_update(nblk, ck_p, tot_p, B_ck_p)
                chunks = [(512 * j, 512, 4 * j, (0 if j == ti else None)) for j in range(ti + 1)]
                attention(T, [dict(q0=0, L=TP, KT=KTp, V=Vp, B_KT=B_KTp, B_V=B_Vp, chunks=chunks,
                                   bias=lambda kb: biasb[:, kb, :], B_bias=B_bias)])
            if "f3" in ST:
                ffn(T, 3)
            final_out(T, [y_p[t0 + 128 * b: t0 + 128 * (b + 1), :] for b in range(4)])
        else:
            T = NS * DSEQ
            load_x(T, [xs[:, :]])
            ffn(T, 0)
            alayer(T, [(0, DSEQ), (DSEQ, DSEQ)], convst_s, hst_s, B_st_s, True, nconv_s, nh_s)
            ffn(T, 1)
            tk.op("dve", lambda: nc.vector.memset(lf[:], 0.0), [B_lf], [B_lf])
            kvphase(T, [(0, 128)],
                    [(DSEQ * s, DSEQ, ck_s[s][:, 8, :], tot_s[s], B_ck_s[s], nlf_s[DSEQ * s:DSEQ * (s + 1), :]) for s in range(NS)],
                    nk_s, nv_s, None,
                    [(KTs[s], B_KTs[s], DSEQ * s, DSEQ, PAST) for s in range(NS)],
                    [(Vs[s], B_Vs[s], 0, DSEQ * s, DSEQ, PAST) for s in range(NS)])
            tk.op("dve", lambda: nc.vector.memset(lf[:], 0.0), [B_lf], [B_lf])
            ffn(T, 2)
            qsegs = []
            biasS = [carve("bS", 61440 + 1024 * s, [128, 9, NH], F32) for s in range(NS)]
            B_bS = [Buf(f"bS{s}") for s in range(NS)]
            arena_users.extend(B_bS)
            for s in range(NS):
                for kb in range(9):
                    vtt(biasS[s][:, kb, :], tot_s[s][:], ck_s[s][:, kb, :], ALU.subtract, [B_ck_s[s]], [B_bS[s]])
                chunks = [(0, 512, 0, None), (512, 512, 4, None), (PAST, DSEQ, 8, 0)]
                qsegs.append(dict(q0=DSEQ * s, L=DSEQ, KT=KTs[s], V=Vs[s], B_KT=B_KTs[s], B_V=B_Vs[s], chunks=chunks,
                                  bias=(lambda kb, s=s: biasS[s][:, kb, :]), B_bias=B_bS[s]))
            attention(T, qsegs)
            ffn(T, 3)
            final_out(T, [y_s[:, :]])

    tk.wait_all("sp", list(OUT_BUFS.values()))
    return nc, tk


_CACHE = {}


def _in_maps(inp):
    f = lambda a: np.ascontiguousarray(np.asarray(a, dtype=np.float32))
    shared = {
        "ffn_norm": f(inp["ffn_norm"]).reshape(4, D),
        "ffn_w_in": f(inp["ffn_w_in"]).reshape(4, D, 2 * DFF),
        "ffn_w_out": f(inp["ffn_w_out"]).reshape(4, DFF, D),
        "a_norm": f(inp["a_norm"]).reshape(1, D),
        "a_w_in": f(inp["a_w_in"]).reshape(D, 2 * DR),
        "a_conv_w": f(inp["a_conv_w"]).reshape(4, DR),
        "a_conv_b": f(inp["a_conv_b"]).reshape(1, DR),
        "a_gate_w": f(inp["a_gate_w"]).reshape(2, 10, 256, 256),
        "a_gate_b": f(inp["a_gate_b"]).reshape(2, DR),
        "a_lambda": f(inp["a_lambda"]).reshape(1, DR),
        "a_w_out": f(inp["a_w_out"]).reshape(DR, D),
        "kv_norm": f(inp["kv_norm"]).reshape(1, D),
        "w_kv": f(inp["w_kv"]),
        "w_f": f(inp["w_f"]),
        "b_f": f(inp["b_f"]).reshape(1, NH),
        "b_norm": f(inp["b_norm"]).reshape(1, D),
        "b_w_qg": f(inp["b_w_qg"]).reshape(D, 2 * D),
        "b_w_o": f(inp["b_w_o"]).reshape(D, D),
        "final_norm": f(inp["final_norm"]).reshape(1, D),
    }
    xp = f(inp["x_prompt"])
    xs = f(inp["x_sample"])
    sc = f(inp["state_conv"])
    sh = f(inp["state_h"])
    ck = f(inp["cache_k"])
    cv = f(inp["cache_v"])
    cl = f(inp["cache_logf"])
    maps = []
    for c in range(N_CORES):
        m = dict(shared)
        m["xp"] = xp[c % 4]
        r = slice(NS * c, NS * c + NS)
        m["xs"] = xs[r].reshape(NS * DSEQ, D)
        m["st_conv"] = sc[0, r].reshape(NS * 3, DR)
        m["st_h"] = sh[0, r].reshape(NS, DR)
        m["cache_k"] = ck[r].reshape(NS, PAST, D)
        m["cache_v"] = cv[r].reshape(NS, PAST, D)
        m["cache_lf"] = cl[r].reshape(NS, PAST, NH)
        maps.append(m)
    return maps


def kernel(**inp):
    if "nc" not in _CACHE:
        _CACHE["nc"] = build_nc()[0]
    nc = _CACHE["nc"]
    res = run_bass_kernel_spmd(nc, _in_maps(inp), core_ids=list(range(N_CORES)))
    R = res.results
    B = 4
    y_prompt = np.stack([R[b]["y_p"] for b in range(B)])
    new_conv_p = np.stack([R[b]["nconv_p"] for b in range(B)])[None]
    new_h_p = np.stack([R[b]["nh_p"].reshape(DR) for b in range(B)])[None]
    new_k_p = np.stack([R[b]["nk_p"] for b in range(B)]).reshape(B, SEQ, NH, HD)
    new_v_p = np.stack([R[b]["nv_p"] for b in range(B)]).reshape(B, SEQ, NH, HD)
    new_lf_p = np.stack([R[b]["nlf_p"] for b in range(B)])
    cat = lambda k, shp: np.concatenate([R[c][k].reshape(shp) for c in range(N_CORES)], axis=0)
    y_sample = cat("y_s", (NS, DSEQ, D))
    new_conv_s = cat("nconv_s", (NS, 3, DR))[None]
    new_h_s = cat("nh_s", (NS, DR))[None]
    new_k_s = cat("nk_s", (NS, DSEQ, NH, HD))
    new_v_s = cat("nv_s", (NS, DSEQ, NH, HD))
    new_lf_s = cat("nlf_s", (NS, DSEQ, NH))
    outs = (y_prompt, y_sample, new_conv_p, new_h_p, new_k_p, new_v_p, new_lf_p, new_conv_s, new_h_s, new_k_s, new_v_s,
            new_lf_s)
    return tuple(np.ascontiguousarray(o, dtype=np.float32) for o in outs)
```

```python
import os
import numpy as np
import concourse.bass as bass
import concourse.mybir as mybir
from concourse.bass_utils import run_bass_kernel_spmd

F32 = mybir.dt.float32
BF16 = mybir.dt.bfloat16
AF = mybir.ActivationFunctionType
ALU = mybir.AluOpType

D = 2048
DC = 16
DFF = 5632
FC = 44
DR = 2560
RC = 20
NH = 16
HD = 128
SEQ = 4096
TP = 512
DSEQ = 64
NS = 2
PAST = 1024
SKEYS = PAST + DSEQ
ATTN_SCALE = HD ** -0.5
EPS = 1e-6
N_CORES = 8
SEM_EPOCH = 20000
WSLOT = 11264
NWSLOT = 2
A_STOP = int(os.environ.get('A_STOP', '99'))
A_VAR = os.environ.get('A_VAR', '')
KV_STOP = int(os.environ.get('KV_STOP', '99'))
KV_VAR = os.environ.get('KV_VAR', 'dve')
FFN_VAR = os.environ.get('FFN_VAR', 'win')


class Buf:
    __slots__ = ("name", "w", "r", "const")

    def __init__(self, name="", const=False):
        self.name = name
        self.w = {}
        self.r = {}
        self.const = const


class _Eng:
    def __init__(self, trk, name, eng, is_pe=False):
        self.trk, self.name, self.eng, self.is_pe = trk, name, eng, is_pe
        self.seen = {}
        self.nsem = 0
        self.new_sem()

    def new_sem(self):
        self.sem = self.trk.nc.alloc_semaphore(f"s_{self.name}_{self.nsem}")
        self.nsem += 1
        self.cnt = 0


class Tracker:
    def __init__(self, nc):
        self.nc = nc
        self.E = {
            "pe": _Eng(self, "pe", nc.tensor, True),
            "act": _Eng(self, "act", nc.scalar),
            "dve": _Eng(self, "dve", nc.vector),
            "pool": _Eng(self, "pool", nc.gpsimd),
            "sp": _Eng(self, "sp", nc.sync),
        }
        self.dma_sems = {}
        self.n_wait = 0
        self.n_ins = 0

    def _deps(self, e, reads, writes):
        need = {}

        def add(k, t, is_waw=False):
            sem, val, src = t
            if src == e.name and (e.is_pe or is_waw):
                return
            if k not in need or need[k][1] < val:
                need[k] = (sem, val)
        for b in reads:
            for k, t in b.w.items():
                add(k, t)
        for b in writes:
            for k, t in b.w.items():
                add(k, t, True)
            for k, t in b.r.items():
                add(k, t)
        for k, (sem, val) in need.items():
            if e.seen.get(k, 0) >= val:
                continue
            e.eng.wait_ge(sem, val)
            e.seen[k] = val
            self.n_wait += 1

    @staticmethod
    def _put(d, t):
        k = t[0].num
        if k not in d or d[k][1] < t[1]:
            d[k] = t

    def _commit(self, t, reads, writes):
        for b in reads:
            if not b.const:
                self._put(b.r, t)
        for b in writes:
            self._put(b.w, t)

    def op(self, ename, fn, reads=(), writes=()):
        e = self.E[ename]
        self._deps(e, reads, writes)
        ins = fn()
        if e.cnt >= SEM_EPOCH:
            e.new_sem()
        ins.then_inc(e.sem, 1)
        e.cnt += 1
        self.n_ins += 1
        self._commit((e.sem, e.cnt, e.name), reads, writes)

    def dma(self, ename, out, in_, reads=(), writes=(), key=None, **kw):
        e = self.E[ename]
        self._deps(e, reads, writes)
        if key is None:
            key = writes[0] if writes else reads[0]
        kk = (ename, id(key))
        ent = self.dma_sems.get(kk)
        if ent is None or ent[1] >= SEM_EPOCH * 16:
            ent = [self.nc.alloc_semaphore(f"d{len(self.dma_sems)}_{self.n_ins}"), 0, key]
            self.dma_sems[kk] = ent
        ins = e.eng.dma_start(out=out, in_=in_, **kw)
        ins.then_inc(ent[0], 16)
        ent[1] += 16
        self.n_ins += 1
        self._commit((ent[0], ent[1], "dma"), reads, writes)

    def wait_all(self, ename, bufs):
        self._deps(self.E[ename], bufs, ())


def build_nc(n_ptiles=SEQ // TP, do_sample=True, stages="x,f0,a,f1,kv,f2,at,f3,out,pre"):
    ST = set(stages.split(","))
    nc = bass.Bass("TRN2", target_bir_lowering=False)
    tk = Tracker(nc)

    def din(name, shape):
        return nc.dram_tensor(name, list(shape), F32, kind="ExternalInput").ap()

    def dout(name, shape):
        return nc.dram_tensor(name, list(shape), F32, kind="ExternalOutput").ap()

    xp = din("xp", [SEQ, D])
    xs = din("xs", [NS * DSEQ, D])
    st_conv = din("st_conv", [NS * 3, DR])
    st_h = din("st_h", [NS, DR])
    cache_k = din("cache_k", [NS, PAST, D])
    cache_v = din("cache_v", [NS, PAST, D])
    cache_lf = din("cache_lf", [NS, PAST, NH])
    ffn_norm = din("ffn_norm", [4, D])
    ffn_w_in = din("ffn_w_in", [4, D, 2 * DFF])
    ffn_w_out = din("ffn_w_out", [4, DFF, D])
    a_norm = din("a_norm", [1, D])
    a_w_in = din("a_w_in", [D, 2 * DR])
    a_conv_w = din("a_conv_w", [4, DR])
    a_conv_b = din("a_conv_b", [1, DR])
    a_gate_w = din("a_gate_w", [2, 10, 256, 256])
    a_gate_b = din("a_gate_b", [2, DR])
    a_lambda = din("a_lambda", [1, DR])
    a_w_out = din("a_w_out", [DR, D])
    kv_norm = din("kv_norm", [1, D])
    w_kv = din("w_kv", [D, 2 * D])
    w_f = din("w_f", [D, NH])
    b_f = din("b_f", [1, NH])
    b_norm = din("b_norm", [1, D])
    b_w_qg = din("b_w_qg", [D, 2 * D])
    b_w_o = din("b_w_o", [D, D])
    final_norm = din("final_norm", [1, D])
    y_p = dout("y_p", [SEQ, D])
    y_s = dout("y_s", [NS * DSEQ, D])
    nconv_p = dout("nconv_p", [3, DR])
    nh_p = dout("nh_p", [1, DR])
    nk_p = dout("nk_p", [SEQ, D])
    nv_p = dout("nv_p", [SEQ, D])
    nlf_p = dout("nlf_p", [SEQ, NH])
    nconv_s = dout("nconv_s", [NS * 3, DR])
    nh_s = dout("nh_s", [NS, DR])
    nk_s = dout("nk_s", [NS * DSEQ, D])
    nv_s = dout("nv_s", [NS * DSEQ, D])
    nlf_s = dout("nlf_s", [NS * DSEQ, NH])
    OUT_BUFS = {}

    def obuf(name):
        if name not in OUT_BUFS:
            OUT_BUFS[name] = Buf(name)
        return OUT_BUFS[name]

    SKP = 1152
    KTp = nc.dram_tensor("KTp", [NH, HD, SEQ], BF16).ap()
    Vp = nc.dram_tensor("Vp", [SEQ, D], BF16).ap()
    KTs = [nc.dram_tensor(f"KTs{s}", [NH, HD, SKP], BF16).ap() for s in range(NS)]
    Vs = [nc.dram_tensor(f"Vs{s}", [SKP, D], BF16).ap() for s in range(NS)]
    B_KTp, B_Vp = Buf("KTp"), Buf("Vp")
    B_KTs = [Buf(f"KTs{s}") for s in range(NS)]
    B_Vs = [Buf(f"Vs{s}") for s in range(NS)]

    def sb(name, shape, dt=F32):
        return nc.alloc_sbuf_tensor(name, list(shape), dt)

    xT = sb("xT", [128, DC, TP]);            B_xT = Buf("xT")
    xn = sb("xn", [128, DC, TP], BF16);      B_xn = Buf("xn")
    tok = [sb(f"tok{i}", [128, D]) for i in range(2)]
    B_tok = [Buf(f"tok{i}") for i in range(2)]
    rstd = sb("rstd", [128, TP]);            B_rstd = Buf("rstd")
    wsl = [sb(f"wsl{i}", [128, WSLOT], BF16) for i in range(NWSLOT)]
    B_wsl = [Buf(f"wsl{i}") for i in range(NWSLOT)]
    gwsl = [sb(f"gwsl{i}", [128, 2, 2, 256], BF16) for i in range(2)]
    B_gwsl = [Buf(f"gwsl{i}") for i in range(2)]
    ARENA = 65536
    arena = sb("arena", [128, ARENA // 4])
    ident = sb("ident", [128, 128]);         B_c = Buf("consts", const=True)
    ones_f = sb("ones_f", [128, 128])
    ones_b = sb("ones_b", [128, 128], BF16)
    tri = sb("tri", [128, 128])
    epsc = sb("epsc", [128, 1])
    onec = sb("onec", [128, 1])
    masks = sb("masks", [128, 4, TP], BF16)
    gn = sb("gn", [128, DC, 8])
    prm = sb("prm", [128, RC, 8])
    nsp8 = sb("nsp8", [128, RC])
    nsp16 = sb("nsp16", [128, RC])
    wfb = sb("wfb", [128, DC, NH], BF16)
    bfb = sb("bfb", [128, NH])
    convst_p = sb("convst_p", [128, RC, 1, 3]); hst_p = sb("hst_p", [128, RC, 1])
    convst_s = sb("convst_s", [128, RC, NS, 3]); hst_s = sb("hst_s", [128, RC, NS])
    B_st_p, B_st_s = Buf("st_p"), Buf("st_s")
    ck_p = sb("ck_p", [128, SEQ // 128, NH]); tot_p = sb("tot_p", [128, NH]); B_ck_p = Buf("ck_p")
    ck_s = [sb(f"ck_s{s}", [128, 9, NH]) for s in range(NS)]
    tot_s = [sb(f"tot_s{s}", [128, NH]) for s in range(NS)]
    B_ck_s = [Buf(f"ck_s{s}") for s in range(NS)]
    biasb = sb("biasb", [128, SEQ // 128, NH]); B_bias = Buf("biasb")
    lf = sb("lf", [128, 4, NH]);              B_lf = Buf("lf")
    lft = sb("lft", [128, NH]);               B_lft = Buf("lft")
    sqs = [sb(f"sq{i}", [128, TP]) for i in range(2)]
    B_sq = [Buf(f"sq{i}") for i in range(2)]
    cst = sb("cst", [128, 64]);               B_cst = Buf("cst")
    zT = sb("zT", [NH, TP]);                  B_zT = Buf("zT")
    stsm = sb("stsm", [64, 128]);             B_stsm = Buf("stsm")

    ps = [nc.alloc_psum_tensor(f"ps{i}", [128, 512], F32) for i in range(8)]
    B_ps = [Buf(f"ps{i}") for i in range(8)]
    rr = {"bank": 0, "w": 0, "gw": 0, "tok": 0, "sq": 0}

    def nbank():
        i = rr["bank"]
        rr["bank"] = (i + 1) % 6
        return ps[i], B_ps[i]

    def carve(name, off_bytes, shape, dt):
        esz = 4 if dt == F32 else 2
        n = int(np.prod(shape[1:]))
        v = arena[:, off_bytes // 4: off_bytes // 4 + (n * esz + 3) // 4]
        if dt != F32:
            v = v.bitcast(dt)
        if len(shape) > 2:
            letters = "abcdefg"[: len(shape) - 1]
            pat = "p (" + " ".join(letters) + ") -> p " + " ".join(letters)
            v = v[:, 0:n].rearrange(pat, **{l: s for l, s in zip(letters, shape[1:])})
        else:
            v = v[:, 0:n]
        return v

    B_arena = Buf("arena")
    arena_users = []

    def phase_bufs(names):
        bufs = [Buf(n) for n in names]
        for b in bufs:
            for old in arena_users:
                for k, t in old.w.items():
                    Tracker._put(b.w, (t[0], t[1], "fence"))
                for k, t in old.r.items():
                    Tracker._put(b.w, (t[0], t[1], "fence"))
        arena_users.extend(bufs)
        if len(arena_users) > 64:
            agg = Buf("agg")
            for old in arena_users:
                for k, t in list(old.w.items()) + list(old.r.items()):
                    Tracker._put(agg.w, (t[0], t[1], "fence"))
            arena_users[:] = [agg] + bufs
        return bufs

    def act(out, in_, func, R, W, **kw):
        tk.op("act", lambda: nc.scalar.activation(out=out, in_=in_, func=func, **kw), R, W)

    def vcopy(out, in_, R, W, eng="dve"):
        if eng == "dve":
            tk.op("dve", lambda: nc.vector.tensor_copy(out=out, in_=in_), R, W)
        else:
            tk.op("act", lambda: nc.scalar.copy(out=out, in_=in_), R, W)

    def vtt(out, a, b, op, R, W):
        tk.op("dve", lambda: nc.vector.tensor_tensor(out=out, in0=a, in1=b, op=op), R, W)

    def vts(out, a, s1, s2, op0, op1, R, W):
        if s2 is None:
            tk.op("dve", lambda: nc.vector.tensor_scalar(out=out, in0=a, scalar1=s1, scalar2=None, op0=op0), R, W)
        else:
            tk.op("dve", lambda: nc.vector.tensor_scalar(out=out, in0=a, scalar1=s1, scalar2=s2, op0=op0, op1=op1), R, W)

    def vstt(out, a, s, b, op0, op1, R, W):
        tk.op("dve", lambda: nc.vector.scalar_tensor_tensor(out=out, in0=a, scalar=s, in1=b, op0=op0, op1=op1), R, W)

    def mmg(out, pairs, R, W, first=True, final=True):
        def f():
            n = len(pairs)
            for i, (l, r) in enumerate(pairs):
                last = nc.tensor.matmul(out, lhsT=l, rhs=r, start=(first and i == 0), stop=(final and i == n - 1))
            return last
        tk.op("pe", f, R, W)

    def wload(parts, kc, ncols):
        i = rr["w"]
        rr["w"] = (i + 1) % NWSLOT
        npart = len(parts)
        v = wsl[i][:, 0: kc * npart * ncols].rearrange("p (c g f) -> p c g f", c=kc, g=npart)
        for g, (W2, c0) in enumerate(parts):
            src = W2.rearrange("(c p) f -> p c f", p=128)[:, :, c0:c0 + ncols]
            tk.dma("pool", v[:, :, g, :], src, writes=[B_wsl[i]])
        return v, B_wsl[i]

    rowst = arena[0:8, 0:DR]
    maskf = arena[:, DR:DR + TP]
    (B_rows, B_mk) = phase_bufs(["rows", "maskf"])
    tk.op("pool", lambda: nc.gpsimd.memset(ones_f[:], 1.0), writes=[B_c])
    tk.op("pool", lambda: nc.gpsimd.memset(ones_b[:], 1.0), writes=[B_c])
    tk.op("pool", lambda: nc.gpsimd.memset(epsc[:], EPS), writes=[B_c])
    tk.op("pool", lambda: nc.gpsimd.memset(onec[:], 1.0), writes=[B_c])
    tk.op("pool", lambda: nc.gpsimd.memset(ident[:], 0.0), writes=[B_c])
    tk.op("pool", lambda: nc.gpsimd.affine_select(out=ident[:], in_=ident[:], pattern=[[-1, 128]], compare_op=ALU.not_equal,
                                                   fill=1.0, base=0, channel_multiplier=1), [B_c], [B_c])
    tk.op("pool", lambda: nc.gpsimd.memset(tri[:], 1.0), writes=[B_c])
    tk.op("pool", lambda: nc.gpsimd.affine_select(out=tri[:], in_=tri[:], pattern=[[1, 128]], compare_op=ALU.is_ge,
                                                   fill=0.0, base=0, channel_multiplier=-1), [B_c], [B_c])
    for kb in range(4):
        tk.op("pool", lambda: nc.gpsimd.memset(maskf[:], 1.0), [B_mk], [B_mk])
        tk.op("pool", lambda kb=kb: nc.gpsimd.affine_select(out=maskf[:], in_=maskf[:], pattern=[[1, TP]], compare_op=ALU.is_ge,
                                                             fill=0.0, base=-128 * kb, channel_multiplier=-1), [B_mk], [B_mk])
        tk.op("pool", lambda kb=kb: nc.gpsimd.tensor_copy(out=masks[:, kb, :], in_=maskf[:]), [B_mk], [B_c])
    tk.op("pool", lambda: nc.gpsimd.memset(convst_p[:], 0.0), writes=[B_st_p])
    tk.op("pool", lambda: nc.gpsimd.memset(hst_p[:], 0.0), writes=[B_st_p])
    tk.op("pool", lambda: nc.gpsimd.memset(tot_p[:], 0.0), writes=[B_ck_p])
    tk.op("pool", lambda: nc.gpsimd.memset(lf[:], 0.0), writes=[B_lf])
    tk.op("pool", lambda: nc.gpsimd.memset(bfb[:], 0.0), writes=[B_c])
    for s in range(NS):
        tk.op("pool", lambda s=s: nc.gpsimd.memset(tot_s[s][:], 0.0), writes=[B_ck_s[s]])


    def rows_to_cols(nrows, nchunks, dst3, dstB):
        bank, Bb = nbank()

        def f():
            for c in range(nchunks):
                last = nc.tensor.transpose(out=bank[:, c * nrows:(c + 1) * nrows], in_=rowst[0:nrows, c * 128:(c + 1) * 128],
                                           identity=ident[0:nrows, 0:nrows])
            return last
        tk.op("pe", f, [B_rows, B_c], [Bb])
        vcopy(dst3, bank[:, 0:nchunks * nrows].rearrange("p (c r) -> p c r", c=nchunks), [Bb], [dstB])

    tk.dma("sp", rowst[0:4, 0:D], ffn_norm, writes=[B_rows])
    for i, a in enumerate([a_norm, kv_norm, b_norm, final_norm]):
        tk.dma("sp", rowst[4 + i:5 + i, 0:D], a, writes=[B_rows])
    rows_to_cols(8, DC, gn[:], B_c)
    tk.dma("sp", rowst[0:4, :], a_conv_w, writes=[B_rows])
    tk.dma("sp", rowst[4:5, :], a_conv_b, writes=[B_rows])
    tk.dma("sp", rowst[5:7, :], a_gate_b, writes=[B_rows])
    tk.dma("sp", rowst[7:8, :], a_lambda, writes=[B_rows])
    rows_to_cols(8, RC, prm[:], B_c)
    act(nsp8[:], prm[:, :, 7], AF.Exp, [B_c], [B_c], scale=-1.0)
    act(nsp8[:], nsp8[:], AF.Ln, [B_c], [B_c], bias=onec[:], scale=1.0)
    vts(nsp16[:], nsp8[:], -16.0, None, ALU.mult, None, [B_c], [B_c])
    vts(nsp8[:], nsp8[:], -8.0, None, ALU.mult, None, [B_c], [B_c])
    tk.dma("pool", wfb[:], w_f.rearrange("(c p) h -> p c h", p=128), writes=[B_c])
    tk.dma("sp", bfb[0:1, :], b_f, reads=[B_c], writes=[B_c])
    bank, Bb = nbank()
    mmg(bank[:, 0:NH], [(ones_f[:], bfb[:])], [B_c], [Bb])
    vcopy(bfb[:], bank[:, 0:NH], [Bb, B_c], [B_c])

    def rms(T, gidx, dst, dstB, dst_f32=False):
        bank, Bb = nbank()
        for dc in range(DC):
            i = rr["sq"]
            rr["sq"] = (i + 1) % 2
            act(sqs[i][:, :T], xT[:, dc, :T], AF.Square, [B_xT], [B_sq[i]])
            tk.op("pe", lambda i=i, dc=dc: nc.tensor.matmul(bank[:, :T], lhsT=ones_f[:], rhs=sqs[i][:, :T], start=(dc == 0),
                                                             stop=(dc == DC - 1)), [B_sq[i], B_c], [Bb])
        act(rstd[:, :T], bank[:, :T], AF.Sqrt, [Bb, B_c], [B_rstd], scale=1.0 / D, bias=epsc[:])
        tk.op("dve", lambda: nc.vector.reciprocal(out=rstd[:, :T], in_=rstd[:, :T]), [B_rstd], [B_rstd])
        for dc in range(DC):
            vstt(dst[:, dc, :T], xT[:, dc, :T], gn[:, dc, gidx:gidx + 1], rstd[:, :T], ALU.mult, ALU.mult,
                 [B_xT, B_rstd, B_c], [dstB])

    def ffn(T, idx):
        (B_hid, B_sg0, B_sg1, B_sg2, B_sg3) = phase_bufs(["hid", "sg0", "sg1", "sg2", "sg3"])
        hid = carve("hid", 0, [128, FC, TP], BF16)
        sg = [carve(f"sg{q}", 45056 + 2048 * q, [128, TP], F32) for q in range(4)]
        B_sg = [B_sg0, B_sg1, B_sg2, B_sg3]
        rms(T, idx, xn, B_xn)
        Win, Wout = ffn_w_in[idx], ffn_w_out[idx]
        if FFN_VAR in ("both", "win"):
            for pq in range(FC // 4):
                wv, Bw = wload([(Win, 512 * pq)], DC, 512)
                for jj in range(4):
                    bg, Bg = nbank()
                    mmg(bg[:, :T], [(wv[:, dc, 0, jj * 128:(jj + 1) * 128], xn[:, dc, :T]) for dc in range(DC)], [Bw, B_xn], [Bg])
                    act(sg[jj][:, :T], bg[:, :T], AF.Silu, [Bg], [B_sg[jj]])
                wv, Bw = wload([(Win, DFF + 512 * pq)], DC, 512)
                for jj in range(4):
                    j = 4 * pq + jj
                    bu, Bu = nbank()
                    mmg(bu[:, :T], [(wv[:, dc, 0, jj * 128:(jj + 1) * 128], xn[:, dc, :T]) for dc in range(DC)], [Bw, B_xn], [Bu])
                    vtt(hid[:, j, :T], sg[jj][:, :T], bu[:, :T], ALU.mult, [B_sg[jj], Bu], [B_hid])
        else:
            for pj in range(FC // 2):
                wv, Bw = wload([(Win, 256 * pj), (Win, DFF + 256 * pj)], DC, 256)
                for jj in range(2):
                    j = 2 * pj + jj
                    bg, Bg = nbank()
                    mmg(bg[:, :T], [(wv[:, dc, 0, jj * 128:(jj + 1) * 128], xn[:, dc, :T]) for dc in range(DC)], [Bw, B_xn], [Bg])
                    bu, Bu = nbank()
                    mmg(bu[:, :T], [(wv[:, dc, 1, jj * 128:(jj + 1) * 128], xn[:, dc, :T]) for dc in range(DC)], [Bw, B_xn], [Bu])
                    act(sg[jj][:, :T], bg[:, :T], AF.Silu, [Bg], [B_sg[jj]])
                    vtt(hid[:, j, :T], sg[jj][:, :T], bu[:, :T], ALU.mult, [B_sg[jj], Bu], [B_hid])
        KH = FC // 2
        if FFN_VAR in ("both", "wout"):
            for pd in range(4):
                accs = [nbank() for _ in range(4)]
                for kh in range(2):
                    wv, Bw = wload([(Wout[kh * KH * 128:(kh + 1) * KH * 128, :], 512 * pd)], KH, 512)
                    for dd in range(4):
                        bo, Bo = accs[dd]
                        mmg(bo[:, :T], [(wv[:, j, 0, dd * 128:(dd + 1) * 128], hid[:, kh * KH + j, :T]) for j in range(KH)], [Bw, B_hid], [Bo],
                            first=(kh == 0), final=(kh == 1))
                for dd in range(4):
                    dc = 4 * pd + dd
                    bo, Bo = accs[dd]
                    vstt(xT[:, dc, :T], bo[:, :T], 0.5, xT[:, dc, :T], ALU.mult, ALU.add, [Bo, B_xT], [B_xT])
        else:
            for pd in range(8):
                wv, Bw = wload([(Wout, 256 * pd)], FC, 256)
                for dd in range(2):
                    dc = 2 * pd + dd
                    bo, Bo = nbank()
                    mmg(bo[:, :T], [(wv[:, j, 0, dd * 128:(dd + 1) * 128], hid[:, j, :T]) for j in range(FC)], [Bw, B_hid], [Bo])
                    vstt(xT[:, dc, :T], bo[:, :T], 0.5, xT[:, dc, :T], ALU.mult, ALU.add, [Bo, B_xT], [B_xT])

    def out_rows(src3, nchunks, nrows, srcB, dst_ap, dstB):
        n = nchunks * nrows
        if len(src3.shape) == 3:
            vcopy(cst[:, 0:n].rearrange("p (c r) -> p c r", c=nchunks), src3, [srcB], [B_cst])
        else:
            vcopy(cst[:, 0:n], src3, [srcB], [B_cst])
        bank, Bb = nbank()
        tk.op("pe", lambda: nc.tensor.transpose(out=bank[0:n, 0:128], in_=cst[:, 0:n], identity=ident[:]), [B_cst, B_c], [Bb])
        vcopy(stsm[0:n, :], bank[0:n, 0:128], [Bb], [B_stsm])
        for c in range(nchunks):
            tk.dma("sp", dst_ap[:, c * 128:(c + 1) * 128], stsm[c * nrows:(c + 1) * nrows, :], reads=[B_stsm], writes=[dstB])

    def alayer(T, segs, convst, hst, B_st, last, conv_out, h_out):
        nseg = len(segs)
        L = segs[0][1]
        names = ["gt", "ub", "uc", "ucb", "r", "i", "a", "a2", "h", "t", "hg"]
        Bd = dict(zip(names, phase_bufs(names)))
        off = 0

        def cv(name, shape, dt):
            nonlocal off
            v = carve(name, off, shape, dt)
            esz = 4 if dt == F32 else 2
            off += ((int(np.prod(shape[1:])) * esz + 63) // 64) * 64
            return v
        gt = cv("gt", [128, 2, T], F32)
        ub = cv("ub", [128, 2, nseg, L + 3], F32)
        uc = cv("uc", [128, 2, nseg, L], F32)
        ucb = cv("ucb", [128, 2, nseg, L], BF16)
        rT = cv("r", [128, 2, nseg, L], F32)
        iT = cv("i", [128, 2, nseg, L], F32)
        aT = cv("a", [128, 2, nseg, L], F32)
        a2 = cv("a2", [128, 2, nseg, L], F32)
        hT = cv("h", [128, 2, nseg, L], F32)
        tT = cv("t", [128, 2, T], F32)
        hg = cv("hg", [128, RC, TP], BF16)
        assert off <= ARENA, off
        rms(T, 4, xn, B_xn)
        for hb in range(10):
            wv, Bw = wload([(a_w_in, 256 * hb), (a_w_in, DR + 256 * hb)], DC, 256)
            gi = rr["gw"]
            rr["gw"] = 1 - gi
            for g in range(2):
                tk.dma("pool", gwsl[gi][:, g], a_gate_w[g, hb].rearrange("(ic p) j -> p ic j", p=128), writes=[B_gwsl[gi]])
            gw = gwsl[gi]
            for cc in range(2):
                c = 2 * hb + cc
                bg, Bg = nbank()
                mmg(bg[:, :T], [(wv[:, dc, 0, cc * 128:(cc + 1) * 128], xn[:, dc, :T]) for dc in range(DC)], [Bw, B_xn], [Bg])
                vcopy(gt[:, cc, :], bg[:, :T], [Bg], [Bd["gt"]], eng="act")
                bu, Bu = nbank()
                mmg(bu[:, :T], [(wv[:, dc, 1, cc * 128:(cc + 1) * 128], xn[:, dc, :T]) for dc in range(DC)], [Bw, B_xn], [Bu])
                vcopy(ub[:, cc, :, 3:3 + L], bu[:, :T].rearrange("p (s l) -> p s l", s=nseg), [Bu], [Bd["ub"]], eng="act")
                if A_STOP < 2:
                    continue
                vcopy(ub[:, cc, :, 0:3], convst[:, c, :, :], [B_st], [Bd["ub"]])
                vts(uc[:, cc], ub[:, cc, :, 3:3 + L], prm[:, c, 3:4], prm[:, c, 4:5], ALU.mult, ALU.add, [Bd["ub"], B_c], [Bd["uc"]])
                for k in range(3):
                    vstt(uc[:, cc], ub[:, cc, :, k:k + L], prm[:, c, k:k + 1], uc[:, cc], ALU.mult, ALU.add,
                         [Bd["ub"], Bd["uc"], B_c], [Bd["uc"]])
                vcopy(convst[:, c, :, :], ub[:, cc, :, L:L + 3], [Bd["ub"]], [B_st])
                vcopy(ucb[:, cc], uc[:, cc], [Bd["uc"]], [Bd["ucb"]])
            if A_STOP < 3:
                continue
            for g, dstT, nm in ((0, rT, "r"), (1, iT, "i")):
                for jc in range(2):
                    bq, Bq = nbank()
                    mmg(bq[:, :T], [(gw[:, g, ic, jc * 128:(jc + 1) * 128], ucb[:, ic].rearrange("p s l -> p (s l)"))
                                    for ic in range(2)], [B_gwsl[gi], Bd["ucb"]], [Bq])
                    act(dstT[:, jc].rearrange("p s l -> p (s l)"), bq[:, :T], AF.Sigmoid, [Bq, B_c], [Bd[nm]],
                        bias=prm[:, 2 * hb + jc, 5 + g:6 + g], scale=1.0)
            for cc in range(2):
                if A_STOP < 4:
                    continue
                c = 2 * hb + cc
                fl = lambda v: v[:, cc].rearrange("p s l -> p (s l)")
                act(fl(aT), fl(rT), AF.Exp, [Bd["r"], B_c], [Bd["a"]], scale=nsp8[:, c:c + 1])
                act(fl(a2), fl(rT), AF.Exp, [Bd["r"], B_c], [Bd["a2"]], scale=nsp16[:, c:c + 1])
                act(fl(a2), fl(a2), AF.Sqrt, [Bd["a2"], B_c], [Bd["a2"]], scale=-1.0, bias=onec[:])
                vtt(fl(iT), fl(iT), fl(uc), ALU.mult, [Bd["i"], Bd["uc"]], [Bd["i"]])
                vtt(fl(iT), fl(iT), fl(a2), ALU.mult, [Bd["i"], Bd["a2"]], [Bd["i"]])
                for s in range(nseg):
                    if A_STOP < 5:
                        continue
                    tk.op("dve", lambda s=s, cc=cc, c=c: nc.vector.tensor_tensor_scan(
                        out=hT[:, cc, s, :], data0=aT[:, cc, s, :], data1=iT[:, cc, s, :], initial=hst[:, c, s:s + 1],
                        op0=ALU.mult, op1=ALU.add), [Bd["a"], Bd["i"], B_st], [Bd["h"]])
                    vcopy(hst[:, c, s:s + 1], hT[:, cc, s, L - 1:L], [Bd["h"]], [B_st])
                if A_STOP < 6:
                    continue
                if A_VAR != "onlyhg":
                    act(tT[:, cc, :], gt[:, cc, :], AF.Square, [Bd["gt"]], [Bd["t"]], scale=0.044715 ** 0.5)
                    vstt(tT[:, cc, :], tT[:, cc, :], 1.0, gt[:, cc, :], ALU.add, ALU.mult, [Bd["t"], Bd["gt"]], [Bd["t"]])
                    act(tT[:, cc, :], tT[:, cc, :], AF.Sigmoid, [Bd["t"]], [Bd["t"]], scale=1.5957691216057308)
                    vtt(tT[:, cc, :], tT[:, cc, :], gt[:, cc, :], ALU.mult, [Bd["t"], Bd["gt"]], [Bd["t"]])
                if A_VAR != "nohg":
                    vtt(hg[:, c, :T], tT[:, cc, :], fl(hT), ALU.mult, [Bd["t"], Bd["h"]], [Bd["hg"]])
        for pd in range(4 if A_STOP >= 7 else 0):
            wv, Bw = wload([(a_w_out, 512 * pd)], RC, 512)
            for dd in range(4):
                dc = 4 * pd + dd
                bo, Bo = nbank()
                mmg(bo[:, :T], [(wv[:, j, 0, dd * 128:(dd + 1) * 128], hg[:, j, :T]) for j in range(RC)], [Bw, Bd["hg"]], [Bo])
                vtt(xT[:, dc, :T], bo[:, :T], xT[:, dc, :T], ALU.add, [Bo, B_xT], [B_xT])
        if last and A_STOP >= 8:
            for s in range(nseg):
                for half in range(2):
                    c0 = 10 * half
                    out_rows(convst[:, c0:c0 + 10, s, :], 10, 3, B_st, conv_out[3 * s:3 * s + 3, c0 * 128:(c0 + 10) * 128],
                             obuf("conv"))
                out_rows(hst[:, :, s], RC, 1, B_st, h_out[s:s + 1, :], obuf("h"))

    def cumsum_block(lfblk, ck_dst, tot, B_ck, trimat=None):
        b1, B1 = nbank()
        mmg(b1[:, 0:NH], [(tri[:], lfblk)], [B_lf, B_c], [B1])
        b2, B2 = nbank()
        mmg(b2[:, 0:NH], [(ones_f[:], lfblk)], [B_lf, B_c], [B2])
        vtt(ck_dst, b1[:, 0:NH], tot[:], ALU.add, [B1, B_ck], [B_ck])
        vtt(tot[:], b2[:, 0:NH], tot[:], ALU.add, [B2, B_ck], [B_ck])

    def kvphase(T, tokblocks, lfblocks, k_out, v_out, lf_out, kt_dsts, v_dsts):
        (B_kst, B_vbf, B_ktst) = phase_bufs(["kst", "vbf", "ktst"])
        nb = len(tokblocks)
        kfT = carve("kst", 0, [128, NH, TP], F32)
        vbf = carve("vbf", 32768, [128, 4, D], BF16)
        ktst = carve("ktst", 49152, [128, NH, TP], BF16)
        rms(T, 5, xn, B_xn)
        for half in range(2):
            for p in range(4):
                wv, Bw = wload([(w_kv, half * D + 512 * p)], DC, 512)
                for hh in range(4):
                    h = 4 * p + hh
                    bo, Bo = nbank()
                    mmg(bo[:, :T], [(wv[:, dc, 0, hh * 128:(hh + 1) * 128], xn[:, dc, :T]) for dc in range(DC)], [Bw, B_xn], [Bo])
                    vcopy(kfT[:, h, :T], bo[:, :T], [Bo], [B_kst], eng="dve" if KV_VAR == "dve" else "act")
                    if half == 0:
                        vcopy(ktst[:, h, :T], bo[:, :T], [Bo], [B_ktst])
            dst = k_out if half == 0 else v_out
            for tb, (c0, n) in enumerate(tokblocks):
                i = rr["tok"]
                rr["tok"] = 1 - i
                for g in range(4):
                    bank, Bb = nbank()

                    def f(g=g, c0=c0, n=n, bank=bank):
                        for j in range(4):
                            last = nc.tensor.transpose(out=bank[0:n, j * 128:(j + 1) * 128], in_=kfT[:, 4 * g + j, c0:c0 + n],
                                                       identity=ident[:])
                        return last
                    tk.op("pe", f, [B_kst, B_c], [Bb])
                    vcopy(tok[i][0:n, 512 * g:512 * (g + 1)], bank[0:n, :], [Bb], [B_tok[i]], eng="dve" if KV_VAR == "dve" else "act")
                    if half == 1:
                        vcopy(vbf[0:n, tb, 512 * g:512 * (g + 1)], bank[0:n, :], [Bb], [B_vbf])
                if KV_STOP >= 2:
                    tk.dma("sp", dst[c0:c0 + n, :], tok[i][0:n, :], reads=[B_tok[i]], writes=[obuf("k" if half == 0 else "v")])
        if KV_STOP < 3:
            return
        for (dram, Bd_, col0, ncol, k0) in kt_dsts:
            tk.dma("sp", dram[:, :, k0:k0 + ncol].rearrange("h p k -> p h k"), ktst[:, :, col0:col0 + ncol], reads=[B_ktst],
                   writes=[Bd_])
        for (dram, Bd_, tb, r0, nrow, k0) in v_dsts:
            tk.dma("sp", dram[k0:k0 + nrow, :], vbf[r0:r0 + nrow, tb, :], reads=[B_vbf], writes=[Bd_])
        for bi, (c0, n, ck_dst, tot, B_ck, lf_rows) in enumerate(lfblocks if KV_STOP >= 4 else []):
            if bi == 0:
                bz, Bz = nbank()
                mmg(bz[0:NH, :T], [(wfb[:, dc, :], xn[:, dc, :T]) for dc in range(DC)], [B_xn, B_c], [Bz])
                vcopy(zT[0:NH, :T], bz[0:NH, :T], [Bz], [B_zT], eng="act")
            bo, Bo = nbank()
            tk.op("pe", lambda bo=bo, c0=c0, n=n: nc.tensor.transpose(out=bo[0:n, 0:NH], in_=zT[0:NH, c0:c0 + n], identity=ident[0:NH, 0:NH]),
                  [B_zT, B_c], [Bo])
            vtt(lft[0:n, :], bo[0:n, 0:NH], bfb[0:n, :], ALU.add, [Bo, B_c], [B_lft])
            act(lft[0:n, :], lft[0:n, :], AF.Exp, [B_lft], [B_lft], scale=-1.0)
            act(lft[0:n, :], lft[0:n, :], AF.Ln, [B_lft, B_c], [B_lft], bias=onec[0:n, :], scale=1.0)
            vts(lf[0:n, bi, :], lft[0:n, :], -1.0, None, ALU.mult, None, [B_lft], [B_lf])
            tk.dma("sp", lf_rows, lf[0:n, bi, :], reads=[B_lf], writes=[obuf("lf")])
            if KV_STOP >= 5:
                cumsum_block(lf[:, bi, :], ck_dst, tot, B_ck)

    def attention(T, qsegs):
        names = ["qT", "sg", "kt0", "kt1", "vv0", "vv1", "pT0", "pT1", "pT2", "rden", "otmp"]
        Bd = dict(zip(names, phase_bufs(names)))
        qT = carve("qT", 0, [128, NH, TP], BF16)
        sg = carve("sg", 16384, [128, NH, TP], F32)
        kts = [carve("kt", 49152 + 1024 * i, [128, 512], BF16) for i in range(2)]
        vvs = [carve("vv", 51200 + 1024 * i, [128, 4, 128], BF16) for i in range(2)]
        pTs = [carve("pT", 53248 + 1024 * i, [128, 512], BF16) for i in range(3)]
        rden = carve("rden", 56320, [128, 512], F32)
        otmp = carve("otmp", 58368, [128, 512], F32)
        rms(T, 6, xn, B_xn)
        for p in range(4):
            wv, Bw = wload([(b_w_qg, 512 * p)], DC, 512)
            for hh in range(4):
                h = 4 * p + hh
                bo, Bo = nbank()
                mmg(bo[:, :T], [(wv[:, dc, 0, hh * 128:(hh + 1) * 128], xn[:, dc, :T]) for dc in range(DC)], [Bw, B_xn], [Bo])
                vcopy(qT[:, h, :T], bo[:, :T], [Bo], [Bd["qT"]])
        for p in range(4):
            wv, Bw = wload([(b_w_qg, D + 512 * p)], DC, 512)
            for hh in range(4):
                h = 4 * p + hh
                bo, Bo = nbank()
                mmg(bo[:, :T], [(wv[:, dc, 0, hh * 128:(hh + 1) * 128], xn[:, dc, :T]) for dc in range(DC)], [Bw, B_xn], [Bo])
                act(sg[:, h, :T], bo[:, :T], AF.Sigmoid, [Bo], [Bd["sg"]])
        og = xn
        cnt = {"kv": 0, "p": 0}
        for h in range(NH):
            for qs in qsegs:
                q0, L = qs["q0"], qs["L"]
                first = True
                nblk_total = sum((n + 127) // 128 for (_, n, _, _) in qs["chunks"])
                done = 0
                for (k0, n, ckblk0, mbase) in qs["chunks"]:
                    si = cnt["kv"] % 2
                    cnt["kv"] += 1
                    nbk = (n + 127) // 128
                    tk.dma("sp", kts[si][:, 0:n], qs["KT"][h, :, k0:k0 + n], reads=[qs["B_KT"]], writes=[Bd[f"kt{si}"]])
                    if n % 128 == 0:
                        tk.dma("sp", vvs[si][:, 0:nbk, :],
                               qs["V"][k0:k0 + n, h * HD:(h + 1) * HD].rearrange("(b p) d -> p b d", p=128),
                               reads=[qs["B_V"]], writes=[Bd[f"vv{si}"]])
                    else:
                        tk.dma("sp", vvs[si][0:n, 0, :], qs["V"][k0:k0 + n, h * HD:(h + 1) * HD],
                               reads=[qs["B_V"]], writes=[Bd[f"vv{si}"]])
                    for kb in range(nbk):
                        nk = min(128, n - 128 * kb)
                        bs, Bs = nbank()
                        mmg(bs[0:nk, 0:L], [(kts[si][:, kb * 128:kb * 128 + nk], qT[:, h, q0:q0 + L])], [Bd[f"kt{si}"], Bd["qT"]], [Bs])
                        pi = cnt["p"] % 3
                        cnt["p"] += 1
                        act(pTs[pi][0:nk, 0:L], bs[0:nk, 0:L], AF.Exp, [Bs, qs["B_bias"]], [Bd[f"pT{pi}"]],
                            bias=qs["bias"](ckblk0 + kb)[0:nk, h:h + 1], scale=ATTN_SCALE)
                        if mbase is not None:
                            vtt(pTs[pi][0:nk, 0:L], pTs[pi][0:nk, 0:L], masks[0:nk, mbase + kb, 0:L], ALU.mult,
                                [Bd[f"pT{pi}"], B_c], [Bd[f"pT{pi}"]])
                        done += 1
                        lastb = done == nblk_total

                        def f(si=si, kb=kb, nk=nk, pi=pi, first=first, lastb=lastb):
                            nc.tensor.matmul(ps[6][:, q0:q0 + L], lhsT=vvs[si][0:nk, kb, :], rhs=pTs[pi][0:nk, 0:L], start=first, stop=lastb)
                            return nc.tensor.matmul(ps[7][:, q0:q0 + L], lhsT=ones_b[0:nk, :], rhs=pTs[pi][0:nk, 0:L], start=first,
                                                    stop=lastb)
                        tk.op("pe", f, [Bd[f"vv{si}"], Bd[f"pT{pi}"], B_c], [B_ps[6], B_ps[7]])
                        first = False
                tk.op("dve", lambda: nc.vector.reciprocal(out=rden[:, 0:L], in_=ps[7][:, q0:q0 + L]), [B_ps[7]], [Bd["rden"]])
                vtt(otmp[:, 0:L], ps[6][:, q0:q0 + L], rden[:, 0:L], ALU.mult, [B_ps[6], Bd["rden"]], [Bd["otmp"]])
                vtt(og[:, h, q0:q0 + L], otmp[:, 0:L], sg[:, h, q0:q0 + L], ALU.mult, [Bd["otmp"], Bd["sg"]], [B_xn])
        for pd in range(4):
            wv, Bw = wload([(b_w_o, 512 * pd)], DC, 512)
            for dd in range(4):
                dc = 4 * pd + dd
                bo, Bo = nbank()
                mmg(bo[:, :T], [(wv[:, j, 0, dd * 128:(dd + 1) * 128], og[:, j, :T]) for j in range(DC)], [Bw, B_xn], [Bo])
                vtt(xT[:, dc, :T], bo[:, :T], xT[:, dc, :T], ALU.add, [Bo, B_xT], [B_xT])

    def load_x(T, src_blocks):
        for tb, src in enumerate(src_blocks):
            i = rr["tok"]
            rr["tok"] = 1 - i
            tk.dma("sp", tok[i][:], src, writes=[B_tok[i]])
            for g in range(4):
                bank, Bb = nbank()

                def f(i=i, g=g, bank=bank):
                    for j in range(4):
                        last = nc.tensor.transpose(out=bank[:, j * 128:(j + 1) * 128], in_=tok[i][:, (4 * g + j) * 128:(4 * g + j + 1) * 128],
                                                   identity=ident[:])
                    return last
                tk.op("pe", f, [B_tok[i], B_c], [Bb])
                vcopy(xT[:, 4 * g:4 * g + 4, tb * 128:(tb + 1) * 128], bank[:, :].rearrange("p (j t) -> p j t", j=4), [Bb], [B_xT],
                      eng="act" if g % 2 else "dve")

    def final_out(T, dst_blocks):
        (B_yT,) = phase_bufs(["yT"])
        yT = carve("yT", 0, [128, DC, TP], F32)
        rms(T, 7, yT, B_yT)
        for tb, dst in enumerate(dst_blocks):
            i = rr["tok"]
            rr["tok"] = 1 - i
            for g in range(4):
                bank, Bb = nbank()

                def f(g=g, tb=tb, bank=bank):
                    for j in range(4):
                        last = nc.tensor.transpose(out=bank[:, j * 128:(j + 1) * 128], in_=yT[:, 4 * g + j, tb * 128:(tb + 1) * 128],
                                                   identity=ident[:])
                    return last
                tk.op("pe", f, [B_yT, B_c], [Bb])
                vcopy(tok[i][:, 512 * g:512 * (g + 1)], bank[:, :], [Bb], [B_tok[i]], eng="act" if g % 2 else "dve")
            tk.dma("sp", dst, tok[i][:], reads=[B_tok[i]], writes=[obuf("y")])

    if do_sample and "pre" in ST:
        tk.dma("sp", rowst[0:6, :], st_conv, reads=[B_rows], writes=[B_rows])
        rows_to_cols(6, RC, convst_s[:].rearrange("p c s k -> p c (s k)"), B_st_s)
        tk.dma("sp", rowst[0:2, :], st_h, reads=[B_rows], writes=[B_rows])
        rows_to_cols(2, RC, hst_s[:], B_st_s)
        for s in range(NS):
            (B_ktc, B_vc, B_lfc) = phase_bufs(["ktc", "vc", "lfc"])
            ktc = carve("ktc", 0, [128, NH, PAST], BF16)
            vc = carve("vc", 32768, [128, 8, D], BF16)
            tk.dma("pool", vc[:], cache_v[s].rearrange("(b p) d -> p b d", p=128), writes=[B_vc])
            tk.dma("sp", Vs[s][0:PAST, :].rearrange("(b p) d -> p b d", p=128), vc[:], reads=[B_vc], writes=[B_Vs[s]])
            for blk in range(8):
                i = rr["tok"]
                rr["tok"] = 1 - i
                tk.dma("sp", tok[i][:], cache_k[s, blk * 128:(blk + 1) * 128, :], writes=[B_tok[i]])
                for g in range(4):
                    bank, Bb = nbank()

                    def f(i=i, g=g, bank=bank):
                        for j in range(4):
                            last = nc.tensor.transpose(out=bank[:, j * 128:(j + 1) * 128],
                                                       in_=tok[i][:, (4 * g + j) * 128:(4 * g + j + 1) * 128], identity=ident[:])
                        return last
                    tk.op("pe", f, [B_tok[i], B_c], [Bb])
                    vcopy(ktc[:, 4 * g:4 * g + 4, blk * 128:(blk + 1) * 128], bank[:, :].rearrange("p (j t) -> p j t", j=4), [Bb],
                          [B_ktc], eng="act" if g % 2 else "dve")
            tk.dma("sp", KTs[s][:, :, 0:PAST].rearrange("h p k -> p h k"), ktc[:], reads=[B_ktc], writes=[B_KTs[s]])
            lfc = ck_s[s]
            tk.dma("sp", lfc[:, 0:8, :], cache_lf[s].rearrange("(b p) h -> p b h", p=128), writes=[B_ck_s[s]])
            for blk in range(8):
                b1, B1 = nbank()
                mmg(b1[:, 0:NH], [(tri[:], lfc[:, blk, :])], [B_ck_s[s], B_c], [B1])
                b2, B2 = nbank()
                mmg(b2[:, 0:NH], [(ones_f[:], lfc[:, blk, :])], [B_ck_s[s], B_c], [B2])
                vtt(lfc[:, blk, :], b1[:, 0:NH], tot_s[s][:], ALU.add, [B1, B_ck_s[s]], [B_ck_s[s]])
                vtt(tot_s[s][:], b2[:, 0:NH], tot_s[s][:], ALU.add, [B2, B_ck_s[s]], [B_ck_s[s]])

    def bias_update(nblk, ck, tot, B_ck):
        for kb in range(nblk):
            vtt(biasb[:, kb, :], tot[:], ck[:, kb, :], ALU.subtract, [B_ck], [B_bias])

    tiles = [("p", i) for i in range(n_ptiles)]
    if do_sample:
        tiles.insert(min(1, len(tiles)), ("s", 0))
    for kind, ti in tiles:
        if kind == "p":
            T = TP
            t0 = ti * TP
            load_x(T, [xp[t0 + 128 * b: t0 + 128 * (b + 1), :] for b in range(4)])
            if "f0" in ST:
                ffn(T, 0)
            if "a" in ST:
                alayer(T, [(0, TP)], convst_p, hst_p, B_st_p, ti == n_ptiles - 1, nconv_p, nh_p)
            if "f1" in ST:
                ffn(T, 1)
            if "kv" in ST:
              kvphase(T, [(128 * b, 128) for b in range(4)],
                    [(128 * b, 128, ck_p[:, 4 * ti + b, :], tot_p, B_ck_p, nlf_p[t0 + 128 * b:t0 + 128 * (b + 1), :]) for b in range(4)],
                    nk_p[t0:t0 + TP, :], nv_p[t0:t0 + TP, :], None,
                    [(KTp, B_KTp, 0, TP, t0)],
                    [(Vp, B_Vp, b, 0, 128, t0 + 128 * b) for b in range(4)])
            if "f2" in ST:
                ffn(T, 2)
            nblk = 4 * (ti + 1)
            if "at" in ST:
                bias_update(nblk, ck_p, tot_p, B_ck_p)
                chunks = [(512 * j, 512, 4 * j, (0 if j == ti else None)) for j in range(ti + 1)]
                attention(T, [dict(q0=0, L=TP, KT=KTp, V=Vp, B_KT=B_KTp, B_V=B_Vp, chunks=chunks,
                                   bias=lambda kb: biasb[:, kb, :], B_bias=B_bias)])
            if "f3" in ST:
                ffn(T, 3)
            final_out(T, [y_p[t0 + 128 * b: t0 + 128 * (b + 1), :] for b in range(4)])
        else:
            T = NS * DSEQ
            load_x(T, [xs[:, :]])
            ffn(T, 0)
            alayer(T, [(0, DSEQ), (DSEQ, DSEQ)], convst_s, hst_s, B_st_s, True, nconv_s, nh_s)
            ffn(T, 1)
            tk.op("dve", lambda: nc.vector.memset(lf[:], 0.0), [B_lf], [B_lf])
            kvphase(T, [(0, 128)],
                    [(DSEQ * s, DSEQ, ck_s[s][:, 8, :], tot_s[s], B_ck_s[s], nlf_s[DSEQ * s:DSEQ * (s + 1), :]) for s in range(NS)],
                    nk_s, nv_s, None,
                    [(KTs[s], B_KTs[s], DSEQ * s, DSEQ, PAST) for s in range(NS)],
                    [(Vs[s], B_Vs[s], 0, DSEQ * s, DSEQ, PAST) for s in range(NS)])
            tk.op("dve", lambda: nc.vector.memset(lf[:], 0.0), [B_lf], [B_lf])
            ffn(T, 2)
            qsegs = []
            biasS = [carve("bS", 61440 + 1024 * s, [128, 9, NH], F32) for s in range(NS)]
            B_bS = [Buf(f"bS{s}") for s in range(NS)]
            arena_users.extend(B_bS)
            for s in range(NS):
                for kb in range(9):
                    vtt(biasS[s][:, kb, :], tot_s[s][:], ck_s[s][:, kb, :], ALU.subtract, [B_ck_s[s]], [B_bS[s]])
                chunks = [(0, 512, 0, None), (512, 512, 4, None), (PAST, DSEQ, 8, 0)]
                qsegs.append(dict(q0=DSEQ * s, L=DSEQ, KT=KTs[s], V=Vs[s], B_KT=B_KTs[s], B_V=B_Vs[s], chunks=chunks,
                                  bias=(lambda kb, s=s: biasS[s][:, kb, :]), B_bias=B_bS[s]))
            attention(T, qsegs)
            ffn(T, 3)
            final_out(T, [y_s[:, :]])

    tk.wait_all("sp", list(OUT_BUFS.values()))
    return nc, tk


_CACHE = {}


def _in_maps(inp):
    f = lambda a: np.ascontiguousarray(np.asarray(a, dtype=np.float32))
    shared = {
        "ffn_norm": f(inp["ffn_norm"]).reshape(4, D),
        "ffn_w_in": f(inp["ffn_w_in"]).reshape(4, D, 2 * DFF),
        "ffn_w_out": f(inp["ffn_w_out"]).reshape(4, DFF, D),
        "a_norm": f(inp["a_norm"]).reshape(1, D),
        "a_w_in": f(inp["a_w_in"]).reshape(D, 2 * DR),
        "a_conv_w": f(inp["a_conv_w"]).reshape(4, DR),
        "a_conv_b": f(inp["a_conv_b"]).reshape(1, DR),
        "a_gate_w": f(inp["a_gate_w"]).reshape(2, 10, 256, 256),
        "a_gate_b": f(inp["a_gate_b"]).reshape(2, DR),
        "a_lambda": f(inp["a_lambda"]).reshape(1, DR),
        "a_w_out": f(inp["a_w_out"]).reshape(DR, D),
        "kv_norm": f(inp["kv_norm"]).reshape(1, D),
        "w_kv": f(inp["w_kv"]),
        "w_f": f(inp["w_f"]),
        "b_f": f(inp["b_f"]).reshape(1, NH),
        "b_norm": f(inp["b_norm"]).reshape(1, D),
        "b_w_qg": f(inp["b_w_qg"]).reshape(D, 2 * D),
        "b_w_o": f(inp["b_w_o"]).reshape(D, D),
        "final_norm": f(inp["final_norm"]).reshape(1, D),
    }
    xp = f(inp["x_prompt"])
    xs = f(inp["x_sample"])
    sc = f(inp["state_conv"])
    sh = f(inp["state_h"])
    ck = f(inp["cache_k"])
    cv = f(inp["cache_v"])
    cl = f(inp["cache_logf"])
    maps = []
    for c in range(N_CORES):
        m = dict(shared)
        m["xp"] = xp[c % 4]
        r = slice(NS * c, NS * c + NS)
        m["xs"] = xs[r].reshape(NS * DSEQ, D)
        m["st_conv"] = sc[0, r].reshape(NS * 3, DR)
        m["st_h"] = sh[0, r].reshape(NS, DR)
        m["cache_k"] = ck[r].reshape(NS, PAST, D)
        m["cache_v"] = cv[r].reshape(NS, PAST, D)
        m["cache_lf"] = cl[r].reshape(NS, PAST, NH)
        maps.append(m)
    return maps


def kernel(**inp):
    if "nc" not in _CACHE:
        _CACHE["nc"] = build_nc()[0]
    nc = _CACHE["nc"]
    res = run_bass_kernel_spmd(nc, _in_maps(inp), core_ids=list(range(N_CORES)))
    R = res.results
    B = 4
    y_prompt = np.stack([R[b]["y_p"] for b in range(B)])
    new_conv_p = np.stack([R[b]["nconv_p"] for b in range(B)])[None]
    new_h_p = np.stack([R[b]["nh_p"].reshape(DR) for b in range(B)])[None]
    new_k_p = np.stack([R[b]["nk_p"] for b in range(B)]).reshape(B, SEQ, NH, HD)
    new_v_p = np.stack([R[b]["nv_p"] for b in range(B)]).reshape(B, SEQ, NH, HD)
    new_lf_p = np.stack([R[b]["nlf_p"] for b in range(B)])
    cat = lambda k, shp: np.concatenate([R[c][k].reshape(shp) for c in range(N_CORES)], axis=0)
    y_sample = cat("y_s", (NS, DSEQ, D))
    new_conv_s = cat("nconv_s", (NS, 3, DR))[None]
    new_h_s = cat("nh_s", (NS, DR))[None]
    new_k_s = cat("nk_s", (NS, DSEQ, NH, HD))
    new_v_s = cat("nv_s", (NS, DSEQ, NH, HD))
    new_lf_s = cat("nlf_s", (NS, DSEQ, NH))
    outs = (y_prompt, y_sample, new_conv_p, new_h_p, new_k_p, new_v_p, new_lf_p, new_conv_s, new_h_s, new_k_s, new_v_s,
            new_lf_s)
    return tuple(np.ascontiguousarray(o, dtype=np.float32) for o in outs)
```
